# Optimizing a Trainium2 kernel written in Bass

```python
import jax, jax.numpy as jnp
from jax import lax
import numpy as np

D_MODEL = 1024
BATCH = 8
SEQ = 4096
DEPTH = 4

GRID_W = 64
CTX_LEN = 256
N_MIXERS = 3
D_FF = 4 * D_MODEL
N_MOD = 6
LN_EPS = 1e-6
DEEPNORM_ALPHA = (2.0 * DEPTH) ** 0.25
DEEPNORM_BETA = (8.0 * DEPTH) ** -0.25

ATT_HEAD_DIM = 128
ATT_HEADS = D_MODEL // ATT_HEAD_DIM
ATT_KV_HEADS = 2
ATT_GROUPS = ATT_HEADS // ATT_KV_HEADS
ATT_QKV = (ATT_HEADS + 2 * ATT_KV_HEADS) * ATT_HEAD_DIM
ATT_BLOCK = 128
ROPE_BASE = 10000.0

NA_HEAD_DIM = 64
NA_HEADS = D_MODEL // NA_HEAD_DIM
NA_WIN_H = 8
NA_WIN_W = 16

RW_HEAD = 64
RW_HEADS = D_MODEL // RW_HEAD
RW_DECAY_LORA = 64
RW_AAA_LORA = 64
RW_GATE_LORA = 128
RW_GN_EPS = 64e-5
N_DIR = 2

N_LAYERS_A = (DEPTH + 2) // 3
N_LAYERS_B = (DEPTH + 1) // 3
N_LAYERS_C = DEPTH // 3

kernel_name = 'hybrid_dit_gqa_natten_rwkv7'


def _layernorm(x, g, b):
    xf = x.astype(jnp.float32)
    mu = jnp.mean(xf, -1, keepdims=True)
    var = jnp.mean(jnp.square(xf - mu), -1, keepdims=True)
    return ((xf - mu) * lax.rsqrt(var + LN_EPS) * g + b).astype(x.dtype)


def _rmsnorm(x, g):
    xf = x.astype(jnp.float32)
    return (xf * lax.rsqrt(jnp.mean(xf * xf, -1, keepdims=True) + LN_EPS) * g).astype(x.dtype)


def _modulation(cond, w, b):
    m = jax.nn.silu(cond) @ w + b
    return jnp.split(m[..., None, :], N_MOD, axis=-1)


def _sqrelu_mlp(h, w1, w2):
    return jnp.square(jax.nn.relu(h @ w1)) @ w2


def _axial_rope(n_tokens, head_dim):
    t = jnp.arange(n_tokens)
    row = (t // GRID_W).astype(jnp.float32)
    col = (t % GRID_W).astype(jnp.float32)
    half = head_dim // 2
    freqs = ROPE_BASE ** (-jnp.arange(0, half, 2, dtype=jnp.float32) / half)
    ang_r = row[:, None] * freqs[None, :]
    ang_c = col[:, None] * freqs[None, :]
    ang = jnp.concatenate([ang_r, ang_r, ang_c, ang_c], axis=-1)
    return jnp.cos(ang), jnp.sin(ang)


def _apply_rope(x, cos, sin):
    x1, x2, x3, x4 = jnp.split(x, 4, axis=-1)
    rot = jnp.concatenate([-x2, x1, -x4, x3], axis=-1)
    return (x * cos[:, None] + rot * sin[:, None]).astype(x.dtype)


def _gqa_attend(qb, k, v):
    b, q_len = qb.shape[:2]
    qg = qb.reshape(b, q_len, ATT_KV_HEADS, ATT_GROUPS, ATT_HEAD_DIM)
    s = jnp.einsum('bqkgd,btkd->bkgqt', qg, k, preferred_element_type=jnp.float32) * (ATT_HEAD_DIM ** -0.5)
    p = jax.nn.softmax(s, axis=-1).astype(v.dtype)
    o = jnp.einsum('bkgqt,btkd->bqkgd', p, v)
    return o.reshape(b, q_len, ATT_HEADS * ATT_HEAD_DIM)


def _gqa_axial_attention(h, hc, wqkv, wo, q_norm, k_norm, want_ctx):
    B, S, _ = h.shape
    split = [ATT_HEADS * ATT_HEAD_DIM, (ATT_HEADS + ATT_KV_HEADS) * ATT_HEAD_DIM]

    def project(u):
        b, t = u.shape[:2]
        q, k, v = jnp.split(u @ wqkv, split, axis=-1)
        q = _rmsnorm(q.reshape(b, t, ATT_HEADS, ATT_HEAD_DIM), q_norm)
        k = _rmsnorm(k.reshape(b, t, ATT_KV_HEADS, ATT_HEAD_DIM), k_norm)
        return q, k, v.reshape(b, t, ATT_KV_HEADS, ATT_HEAD_DIM)

    q, k, v = project(h)
    qc, kc, vc = project(hc)
    cos, sin = _axial_rope(S, ATT_HEAD_DIM)
    q = _apply_rope(q, cos, sin)
    k = _apply_rope(k, cos, sin)
    k_all = jnp.concatenate([k, kc], axis=1)
    v_all = jnp.concatenate([v, vc], axis=1)
    n_blk = S // ATT_BLOCK
    q_blk = jnp.moveaxis(q.reshape(B, n_blk, ATT_BLOCK, ATT_HEADS, ATT_HEAD_DIM), 1, 0)
    o = lax.map(lambda qb: _gqa_attend(qb, k_all, v_all), q_blk)
    y = jnp.moveaxis(o, 0, 1).reshape(B, S, ATT_HEADS * ATT_HEAD_DIM) @ wo
    yc = _gqa_attend(qc, kc, vc) @ wo if want_ctx else None
    return y, yc


def _neighbourhood_attention(h, hc, wqkv, wo, rpb, want_ctx):
    B, S, _ = h.shape
    rows = S // GRID_W
    wr = min(NA_WIN_H, rows)
    scale = NA_HEAD_DIM ** -0.5

    def project(u):
        shp = u.shape[:2] + (NA_HEADS, NA_HEAD_DIM)
        q, k, v = jnp.split(u @ wqkv, 3, axis=-1)
        return q.reshape(shp), k.reshape(shp), v.reshape(shp)

    q, k, v = project(h)
    qc, kc, vc = project(hc)
    grid = (B, rows, GRID_W, NA_HEADS, NA_HEAD_DIM)
    k_grid, v_grid = k.reshape(grid), v.reshape(grid)
    q_rows = jnp.moveaxis(q.reshape(grid), 1, 0)
    col = jnp.arange(GRID_W)
    col_idx = jnp.clip(col - NA_WIN_W // 2, 0, GRID_W - NA_WIN_W)[:, None] + jnp.arange(NA_WIN_W)[None, :]
    dc = col_idx - col[:, None] + (NA_WIN_W - 1)
    n_loc = wr * NA_WIN_W

    def row_block(args):
        r, qr = args
        r0 = jnp.clip(r - wr // 2, 0, rows - wr)
        k_win = lax.dynamic_slice_in_dim(k_grid, r0, wr, axis=1)[:, :, col_idx]
        v_win = lax.dynamic_slice_in_dim(v_grid, r0, wr, axis=1)[:, :, col_idx]
        dr = r0 + jnp.arange(wr) - r + (NA_WIN_H - 1)
        bias = jnp.transpose(rpb[:, dr[:, None, None], dc[None, :, :]], (0, 2, 1, 3))
        s_loc = jnp.einsum('bqhd,bjqwhd->bhqjw', qr, k_win, preferred_element_type=jnp.float32) * scale + bias
        s_ctx = jnp.einsum('bqhd,blhd->bhql', qr, kc, preferred_element_type=jnp.float32) * scale
        p = jax.nn.softmax(jnp.concatenate([s_loc.reshape(B, NA_HEADS, GRID_W, n_loc), s_ctx], axis=-1), axis=-1).astype(v.dtype)
        p_loc = p[..., :n_loc].reshape(B, NA_HEADS, GRID_W, wr, NA_WIN_W)
        o = jnp.einsum('bhqjw,bjqwhd->bqhd', p_loc, v_win) + jnp.einsum('bhql,blhd->bqhd', p[..., n_loc:], vc)
        return o.reshape(B, GRID_W, D_MODEL)

    o = lax.map(row_block, (jnp.arange(rows), q_rows))
    y = jnp.moveaxis(o, 0, 1).reshape(B, S, D_MODEL) @ wo
    yc = None
    if want_ctx:
        sc = jnp.einsum('bqhd,blhd->bhql', qc, kc, preferred_element_type=jnp.float32) * scale
        oc = jnp.einsum('bhql,blhd->bqhd', jax.nn.softmax(sc, axis=-1).astype(vc.dtype), vc)
        yc = oc.reshape(B, hc.shape[1], D_MODEL) @ wo
    return y, yc


def _token_shift_centred(x):
    prev = jnp.pad(x[:, :-1], ((0, 0), (1, 0), (0, 0)))
    nxt = jnp.pad(x[:, 1:], ((0, 0), (0, 1), (0, 0)))
    return 0.5 * (prev + nxt) - x


def _time_major_bidir(u):
    u = jnp.stack([u[0], jnp.flip(u[1], axis=1)])
    return jnp.moveaxis(u, 2, 0).astype(jnp.float32)


def _rwkv_prepare(h, p):
    B, T, _ = h.shape
    hs = (B, T, RW_HEADS, RW_HEAD)
    dhs = (N_DIR,) + hs
    xx = _token_shift_centred(h)
    xr, xw, xk, xv, xa, xg = [h + xx * p['mu'][j] for j in range(6)]
    r = (xr @ p['wr']).reshape(hs)
    k = xk @ p['wk']
    v = (xv @ p['wv']).reshape(hs)
    w = p['w0'][:, None, None] + jnp.einsum('nbtr,nrd->nbtd', jnp.tanh(jnp.einsum('btd,ndr->nbtr', xw, p['w1'])), p['w2'])
    w = -jax.nn.softplus(-w.astype(jnp.float32)) - 0.5
    a = jax.nn.sigmoid(p['a0'][:, None, None] + jnp.einsum('nbtr,nrd->nbtd', jnp.einsum('btd,ndr->nbtr', xa, p['a1']), p['a2']))
    g = jax.nn.sigmoid(xg @ p['g1']) @ p['g2']
    kk = (k * p['k_k']).reshape(hs).astype(jnp.float32)
    kk = kk * lax.rsqrt(jnp.maximum(jnp.sum(kk * kk, -1, keepdims=True), 1e-12))
    k_dir = (k[None] * (1.0 + (a - 1.0) * p['k_a'])).reshape(dhs)
    a = a.reshape(dhs)
    decay = jnp.exp(-jnp.exp(w)).reshape(dhs)
    both = lambda u: jnp.broadcast_to(u, dhs)
    scan_in = (_time_major_bidir(both(r)), _time_major_bidir(decay), _time_major_bidir(k_dir),
               _time_major_bidir(both(v)), _time_major_bidir(both(-kk)), _time_major_bidir(kk[None] * a))
    return scan_in, r, k_dir, v, g


def _rwkv_scan(scan_in, s0, emit):
    def step(s, inp):
        r_t, w_t, k_t, v_t, a_t, b_t = inp
        sa = jnp.einsum('nbhij,nbhj->nbhi', s, a_t)
        s = s * w_t[..., None, :] + sa[..., None] * b_t[..., None, :] + v_t[..., None] * k_t[..., None, :]
        return s, (jnp.einsum('nbhij,nbhj->nbhi', s, r_t) if emit else None)
    return lax.scan(step, s0, scan_in)


def _rwkv_output(ys, r, k_dir, v, g, p):
    y = ys[:, 0] + jnp.flip(ys[:, 1], axis=0)
    y = jnp.moveaxis(y, 0, 1)
    mu = jnp.mean(y, -1, keepdims=True)
    var = jnp.mean(jnp.square(y - mu), -1, keepdims=True)
    y = (y - mu) * lax.rsqrt(var + RW_GN_EPS) * p['lnx_g'].reshape(RW_HEADS, RW_HEAD) + p['lnx_b'].reshape(RW_HEADS, RW_HEAD)
    bonus = jnp.sum(jnp.sum(r[None] * k_dir * p['r_k'], -1, keepdims=True), axis=0) * v
    y = (y + bonus).astype(g.dtype).reshape(g.shape)
    return (y * g) @ p['wo']


def _bidir_rwkv7(h, hc, p, want_ctx):
    B = h.shape[0]
    ctx_in, rc, kc, vc, gc = _rwkv_prepare(hc, p)
    lat_in, r, k, v, g = _rwkv_prepare(h, p)
    s0 = jnp.zeros((N_DIR, B, RW_HEADS, RW_HEAD, RW_HEAD), jnp.float32)
    s_ctx, ys_ctx = _rwkv_scan(ctx_in, s0, want_ctx)
    _, ys = _rwkv_scan(lat_in, s_ctx, True)
    y = _rwkv_output(ys, r, k, v, g, p)
    yc = _rwkv_output(ys_ctx, rc, kc, vc, gc, p) if want_ctx else None
    return y, yc


def setup_inputs(seed: int = 0) -> dict:
    key = jax.random.key(seed)
    keys = iter(jax.random.split(key, 64))

    def nrm(shape, scale):
        return jax.random.normal(next(keys), shape, jnp.float32) * scale

    D = D_MODEL
    sD = D ** -0.5
    return {
        'x': nrm((BATCH, SEQ, D), 1.0),
        'c': nrm((BATCH, D), 1.0),
        'ctx': nrm((BATCH, CTX_LEN, D), 1.0),
        'c_ctx': nrm((D,), 1.0),
        'mod_w': nrm((DEPTH, D, N_MOD * D), 0.5 * sD),
        'mod_b': nrm((DEPTH, N_MOD * D), 0.01),
        'post_ln_g': 1.0 + nrm((DEPTH, 2, D), 0.02),
        'post_ln_b': nrm((DEPTH, 2, D), 0.02),
        'mlp_w1': nrm((DEPTH, D, D_FF), sD),
        'mlp_w2': nrm((DEPTH, D_FF, D), D_FF ** -0.5 * DEEPNORM_BETA),
        'att_wqkv': nrm((N_LAYERS_A, D, ATT_QKV), sD),
        'att_wo': nrm((N_LAYERS_A, ATT_HEADS * ATT_HEAD_DIM, D), sD * DEEPNORM_BETA),
        'att_q_norm': 1.0 + nrm((N_LAYERS_A, ATT_HEAD_DIM), 0.02),
        'att_k_norm': 1.0 + nrm((N_LAYERS_A, ATT_HEAD_DIM), 0.02),
        'na_wqkv': nrm((N_LAYERS_B, D, 3 * D), sD),
        'na_wo': nrm((N_LAYERS_B, D, D), sD * DEEPNORM_BETA),
        'na_rpb': nrm((N_LAYERS_B, NA_HEADS, 2 * NA_WIN_H - 1, 2 * NA_WIN_W - 1), 0.1),
        'rw_mu': jax.random.uniform(next(keys), (N_LAYERS_C, 6, D), jnp.float32),
        'rw_wr': nrm((N_LAYERS_C, D, D), sD),
        'rw_wk': nrm((N_LAYERS_C, D, D), sD),
        'rw_wv': nrm((N_LAYERS_C, D, D), sD),
        'rw_wo': nrm((N_LAYERS_C, D, D), sD * DEEPNORM_BETA),
        'rw_w0': -2.0 + nrm((N_LAYERS_C, N_DIR, D), 0.5),
        'rw_w1': nrm((N_LAYERS_C, N_DIR, D, RW_DECAY_LORA), sD),
        'rw_w2': nrm((N_LAYERS_C, N_DIR, RW_DECAY_LORA, D), 0.5 * RW_DECAY_LORA ** -0.5),
        'rw_a0': nrm((N_LAYERS_C, N_DIR, D), 0.5),
        'rw_a1': nrm((N_LAYERS_C, N_DIR, D, RW_AAA_LORA), sD),
        'rw_a2': nrm((N_LAYERS_C, N_DIR, RW_AAA_LORA, D), 0.5 * RW_AAA_LORA ** -0.5),
        'rw_g1': nrm((N_LAYERS_C, D, RW_GATE_LORA), sD),
        'rw_g2': nrm((N_LAYERS_C, RW_GATE_LORA, D), RW_GATE_LORA ** -0.5),
        'rw_k_k': 0.85 + nrm((N_LAYERS_C, D), 0.02),
        'rw_k_a': 1.0 + nrm((N_LAYERS_C, D), 0.02),
        'rw_r_k': nrm((N_LAYERS_C, RW_HEADS, RW_HEAD), 0.1),
        'rw_lnx_g': 1.0 + nrm((N_LAYERS_C, D), 0.02),
        'rw_lnx_b': nrm((N_LAYERS_C, D), 0.02),
    }


def reference(x, c, ctx, c_ctx, mod_w, mod_b, post_ln_g, post_ln_b, mlp_w1, mlp_w2,
              att_wqkv, att_wo, att_q_norm, att_k_norm, na_wqkv, na_wo, na_rpb,
              rw_mu, rw_wr, rw_wk, rw_wv, rw_wo, rw_w0, rw_w1, rw_w2, rw_a0, rw_a1, rw_a2,
              rw_g1, rw_g2, rw_k_k, rw_k_a, rw_r_k, rw_lnx_g, rw_lnx_b):
    xc = ctx
    for i in range(DEPTH):
        kind, slot = i % N_MIXERS, i // N_MIXERS
        want_ctx = i < DEPTH - 1
        sh1, sc1, gt1, sh2, sc2, gt2 = _modulation(c, mod_w[i], mod_b[i])
        csh1, csc1, cgt1, csh2, csc2, cgt2 = _modulation(c_ctx, mod_w[i], mod_b[i])
        h = x * (1.0 + sc1) + sh1
        hc = xc * (1.0 + csc1) + csh1
        if kind == 0:
            y, yc = _gqa_axial_attention(h, hc, att_wqkv[slot], att_wo[slot], att_q_norm[slot], att_k_norm[slot], want_ctx)
        elif kind == 1:
            y, yc = _neighbourhood_attention(h, hc, na_wqkv[slot], na_wo[slot], na_rpb[slot], want_ctx)
        else:
            p = {'mu': rw_mu[slot], 'wr': rw_wr[slot], 'wk': rw_wk[slot], 'wv': rw_wv[slot], 'wo': rw_wo[slot],
                 'w0': rw_w0[slot], 'w1': rw_w1[slot], 'w2': rw_w2[slot], 'a0': rw_a0[slot], 'a1': rw_a1[slot],
                 'a2': rw_a2[slot], 'g1': rw_g1[slot], 'g2': rw_g2[slot], 'k_k': rw_k_k[slot], 'k_a': rw_k_a[slot],
                 'r_k': rw_r_k[slot], 'lnx_g': rw_lnx_g[slot], 'lnx_b': rw_lnx_b[slot]}
            y, yc = _bidir_rwkv7(h, hc, p, want_ctx)
        x = _layernorm(DEEPNORM_ALPHA * x + gt1 * y, post_ln_g[i, 0], post_ln_b[i, 0])
        x = _layernorm(DEEPNORM_ALPHA * x + gt2 * _sqrelu_mlp(x * (1.0 + sc2) + sh2, mlp_w1[i], mlp_w2[i]),
                       post_ln_g[i, 1], post_ln_b[i, 1])
        if want_ctx:
            xc = _layernorm(DEEPNORM_ALPHA * xc + cgt1 * yc, post_ln_g[i, 0], post_ln_b[i, 0])
            xc = _layernorm(DEEPNORM_ALPHA * xc + cgt2 * _sqrelu_mlp(xc * (1.0 + csc2) + csh2, mlp_w1[i], mlp_w2[i]),
                            post_ln_g[i, 1], post_ln_b[i, 1])
    return x
```

```python
import os
import numpy as np
from contextlib import ExitStack
import concourse.bass as bass
import concourse.mybir as mybir
from concourse.bass_utils import run_bass_kernel_spmd

F32 = mybir.dt.float32
BF16 = mybir.dt.bfloat16
AF = mybir.ActivationFunctionType
ALU = mybir.AluOpType
AX = mybir.AxisListType

D = 1024
NLAT = 4096
NCTX = 256
T = NLAT + NCTX
NT = T // 128
DEPTH = 4
DFF = 4096
ALPHA = (2.0 * DEPTH) ** 0.25
LN_EPS = 1e-6
GRID_W = 64

EPOCH = 8000
NSLOT = 12


class Buf:
    __slots__ = ("ap", "w", "r", "name", "excl")

    def __init__(self, ap, name=""):
        self.ap = ap
        self.w = None
        self.r = {}
        self.name = name
        self.excl = False

    def __getitem__(self, k):
        return self.ap[k]


class Prog:
    ENGS = ("pe", "act", "dve", "pool", "sp")

    def __init__(self, nc):
        self.nc = nc
        self.q = {e: [] for e in self.ENGS}
        self.cnt = {e: 0 for e in self.ENGS}
        self.know = {e: {} for e in self.ENGS}
        self.dma_rr = {e: 0 for e in self.ENGS}
        self.dma_val = {}
        self.pending = {}

    def barrier(self):
        toks = [("E", e, self.cnt[e]) for e in self.ENGS if self.cnt[e]]
        toks += [("D", e, s, v) for (e, s), v in self.dma_val.items()]
        self.pending = {e: list(toks) for e in self.ENGS}

    def _op(self, eng, fn, reads, writes, dma, fence=False):
        deps = []
        if eng in self.pending:
            deps.extend(self.pending.pop(eng))
        for b in reads:
            if b.w is not None:
                deps.append(b.w)
            if b.excl:
                deps.extend(t for k_, t in b.r.items() if k_ != ("E", eng))
        for b in writes:
            if b.w is not None:
                deps.append(b.w)
            deps.extend(b.r.values())
        slot = None
        if dma:
            slot = self.dma_rr[eng]
            self.dma_rr[eng] = (slot + 1) % NSLOT
            prev = self.dma_val.get((eng, slot), 0)
            if prev:
                deps.append(("D", eng, slot, prev))
        waits = []
        kn = self.know[eng]
        for t in deps:
            if t[0] == "E":
                if t[1] == eng and eng == "pe":
                    continue
                key = ("E", t[1])
                val = t[2]
            else:
                key = ("D", t[1], t[2])
                val = t[3]
            if kn.get(key, 0) >= val:
                continue
            kn[key] = val
            waits.append((key, val))
        if fence and self.cnt[eng] and kn.get(("E", eng), 0) < self.cnt[eng]:
            kn[("E", eng)] = self.cnt[eng]
            waits.append((("E", eng), self.cnt[eng]))
        if dma:
            val = self.dma_val.get((eng, slot), 0) + 16
            self.dma_val[(eng, slot)] = val
            tok = ("D", eng, slot, val)
            rkey = ("D", eng, slot)
        else:
            self.cnt[eng] += 1
            tok = ("E", eng, self.cnt[eng])
            rkey = ("E", eng)
        self.q[eng].append((waits, fn, tok))
        for b in reads:
            b.r[rkey] = tok
        for b in writes:
            b.w = tok
            b.r = {}
        return tok

    def pe(self, fn, reads=(), writes=(), fence=False):
        return self._op("pe", fn, reads, writes, False, fence)

    def act(self, fn, reads=(), writes=()):
        return self._op("act", fn, reads, writes, False)

    def dve(self, fn, reads=(), writes=()):
        return self._op("dve", fn, reads, writes, False)

    def pool(self, fn, reads=(), writes=()):
        return self._op("pool", fn, reads, writes, False)

    def dma(self, out_ap, in_ap, reads=(), writes=(), q="sp"):
        return self._op(q, lambda e: e.dma_start(out=out_ap, in_=in_ap), reads, writes, True)

    def finish(self, stack):
        nc = self.nc
        sems = {}

        def sem_for(key, val):
            if key[0] == "E":
                k = (val - 1) // EPOCH
                skey = ("E", key[1], k)
                v = val - k * EPOCH
            else:
                skey = key
                v = val
            if skey not in sems:
                sems[skey] = stack.enter_context(nc.semaphore("s_" + "_".join(str(x) for x in skey)))
            return sems[skey], v

        final = []
        for e in self.ENGS:
            if self.cnt[e]:
                final.append((("E", e), self.cnt[e]))
        for (e, slot), v in self.dma_val.items():
            final.append((("D", e, slot), v))
        finalw = [(k, v) for (k, v) in final if self.know["sp"].get(k, 0) < v]
        block = stack.enter_context(nc.Block())
        prog = self

        def replay(eng_name, engine):
            for waits, fn, tok in prog.q[eng_name]:
                for key, val in waits:
                    s, v = sem_for(key, val)
                    engine.wait_ge(s, v)
                ins = fn(engine)
                if tok[0] == "E":
                    s, v = sem_for(("E", tok[1]), tok[2])
                    ins.then_inc(s, 1)
                else:
                    s, v = sem_for(("D", tok[1], tok[2]), tok[3])
                    ins.then_inc(s, 16)
            if eng_name == "sp":
                for key, val in finalw:
                    s, v = sem_for(key, val)
                    engine.wait_ge(s, v)

        @block.tensor
        def _(e):
            replay("pe", e)

        @block.scalar
        def _(e):
            replay("act", e)

        @block.vector
        def _(e):
            replay("dve", e)

        @block.gpsimd
        def _(e):
            replay("pool", e)

        @block.sync
        def _(e):
            replay("sp", e)


class Arena:
    def __init__(self, ap, nwords):
        self.ap = ap
        self.n = nwords
        self.off = 0

    def reset(self):
        self.off = 0

    def alloc(self, shape, dt, name=""):
        free = 1
        for s in shape[1:]:
            free *= s
        words = free if dt == F32 else (free + 1) // 2
        words = (words + 7) // 8 * 8
        assert self.off + words <= self.n, f"arena overflow {name} {self.off}+{words}>{self.n}"
        v = self.ap[:, self.off:self.off + words]
        self.off += words
        if dt != F32:
            v = v.bitcast(dt)
        v = v[:, 0:free]
        if len(shape) == 3:
            v = v.rearrange("p (a b) -> p a b", a=shape[1])
        elif len(shape) == 4:
            v = v.rearrange("p (a b c) -> p a b c", a=shape[1], b=shape[2])
        return Buf(v, name)

    def ring(self, n, shape, dt, name=""):
        return [self.alloc(shape, dt, f"{name}{i}") for i in range(n)]


def rope_tables():
    t = np.arange(NLAT)
    row = (t // GRID_W).astype(np.float32)
    col = (t % GRID_W).astype(np.float32)
    half = 64
    freqs = (np.float32(10000.0) ** (-np.arange(0, half, 2, dtype=np.float32) / np.float32(half))).astype(np.float32)
    ang_r = row[:, None] * freqs[None, :]
    ang_c = col[:, None] * freqs[None, :]
    ang = np.concatenate([ang_r, ang_r, ang_c, ang_c], axis=-1).astype(np.float32)
    cos = np.cos(ang).astype(np.float32).T.copy()
    sin = np.sin(ang).astype(np.float32).T.copy()
    sign = np.ones((128, 1), np.float32)
    sign[0:32] = -1.0
    sign[64:96] = -1.0
    return cos, (sin * sign).astype(np.float32)


def na_bias_table(rpb):
    NEG = np.float32(-30000.0)
    out = np.full((16, 5, 128, 640), NEG, np.float32)
    ps = [0, 1, 2, 30, 31]
    c = np.arange(64)
    c0 = np.clip(c - 8, 0, 48)
    cp = np.arange(64)
    valid_c = (cp[None, :] >= c0[:, None]) & (cp[None, :] < c0[:, None] + 16)
    dc = np.clip(cp[None, :] - c[:, None] + 15, 0, 30)
    for ti, p in enumerate(ps):
        ws = min(max(2 * p - 4, 0), 54)
        for rl in range(2):
            r = 2 * p + rl
            r0 = min(max(r - 4, 0), 56)
            for j in range(10):
                kr = ws + j
                if not (r0 <= kr < r0 + 8):
                    continue
                dr = kr - r + 7
                vals = rpb[:, dr, :][:, dc]
                blk = np.where(valid_c[None], vals, NEG)
                out[:, ti, rl * 64:(rl + 1) * 64, j * 64:(j + 1) * 64] = blk
    return out


def rwkv_consts():
    u = np.arange(128)
    same = (u[:, None] // 64) == (u[None, :] // 64)
    out = np.zeros((128, 1282), np.float32)
    for d in range(2):
        le = (u[:, None] <= u[None, :]) if d == 0 else (u[:, None] >= u[None, :])
        lt = (u[:, None] < u[None, :]) if d == 0 else (u[:, None] > u[None, :])
        o = d * 640
        out[:, o:o + 128] = same & le
        out[:, o + 128:o + 256] = same & lt.T
        out[:, o + 256:o + 384] = same & lt
        out[:, o + 384:o + 512] = same & le
        out[:, o + 512:o + 640] = same & lt.T
    out[:, 1280] = (u < 64)
    out[:, 1281] = (u >= 64)
    return out


W_NAMES = ["mod_w", "mod_b", "post_ln_g", "post_ln_b", "mlp_w1", "mlp_w2", "att_wqkv", "att_wo", "att_q_norm",
           "att_k_norm", "na_wqkv", "na_wo", "na_rpb", "rw_mu", "rw_wr", "rw_wk", "rw_wv", "rw_wo", "rw_w0", "rw_w1",
           "rw_w2", "rw_a0", "rw_a1", "rw_a2", "rw_g1", "rw_g2", "rw_k_k", "rw_k_a", "rw_r_k", "rw_lnx_g", "rw_lnx_b"]
W_SHAPES = {
    "mod_w": [4, 1024, 6144], "mod_b": [4, 6144], "post_ln_g": [4, 2, 1024], "post_ln_b": [4, 2, 1024],
    "mlp_w1": [4, 1024, 4096], "mlp_w2": [4, 4096, 1024], "att_wqkv": [2, 1024, 1536], "att_wo": [2, 1024, 1024],
    "att_q_norm": [2, 128], "att_k_norm": [2, 128], "na_wqkv": [1, 1024, 3072], "na_wo": [1, 1024, 1024],
    "na_rpb": [1, 16, 15, 31], "rw_mu": [1, 6, 1024], "rw_wr": [1, 1024, 1024], "rw_wk": [1, 1024, 1024],
    "rw_wv": [1, 1024, 1024], "rw_wo": [1, 1024, 1024], "rw_w0": [1, 2, 1024], "rw_w1": [1, 2, 1024, 64],
    "rw_w2": [1, 2, 64, 1024], "rw_a0": [1, 2, 1024], "rw_a1": [1, 2, 1024, 64], "rw_a2": [1, 2, 64, 1024],
    "rw_g1": [1, 1024, 128], "rw_g2": [1, 128, 1024], "rw_k_k": [1, 1024], "rw_k_a": [1, 1024],
    "rw_r_k": [1, 16, 64], "rw_lnx_g": [1, 1024], "rw_lnx_b": [1, 1024],
}


def build(n_layers=DEPTH, debug=False, rw_only=False, rw_stop=99):
    nc = bass.Bass("TRN2", target_bir_lowering=False)
    din = {}
    x_in = nc.dram_tensor("x", [NLAT, D], F32, kind="ExternalInput").ap()
    ctx_in = nc.dram_tensor("ctx", [NCTX, D], F32, kind="ExternalInput").ap()
    cc_in = nc.dram_tensor("cc", [2, D], F32, kind="ExternalInput").ap()
    for n in W_NAMES:
        din[n] = nc.dram_tensor(n, W_SHAPES[n], F32, kind="ExternalInput").ap()
    rope_cos = nc.dram_tensor("rope_cos", [128, NLAT], F32, kind="ExternalInput").ap()
    rope_sin = nc.dram_tensor("rope_sin", [128, NLAT], F32, kind="ExternalInput").ap()
    na_bias = nc.dram_tensor("na_bias", [16, 5, 128, 640], F32, kind="ExternalInput").ap()
    rwc_in = nc.dram_tensor("rwc", [128, 1282], F32, kind="ExternalInput").ap()
    y_out = nc.dram_tensor("y", [NLAT, D], F32, kind="ExternalOutput").ap()
    dbg_out = nc.dram_tensor("dbg", [T, D], F32, kind="ExternalOutput").ap() if debug else None

    xbuf = [nc.dram_tensor(f"xs{i}", [T, D], F32, kind="Internal").ap() for i in range(2)]
    x1s = nc.dram_tensor("x1s", [T, D], F32, kind="Internal").ap()
    mscr = nc.dram_tensor("mscr", [DEPTH, 2, 6 * D], F32, kind="Internal").ap()
    w1b = nc.dram_tensor("w1b", [D, DFF], BF16, kind="Internal").ap()
    w2b = nc.dram_tensor("w2b", [DFF, D], BF16, kind="Internal").ap()
    wqkvb = nc.dram_tensor("wqkvb", [D, 3072], BF16, kind="Internal").ap()
    wob = nc.dram_tensor("wob", [D, D], BF16, kind="Internal").ap()
    qT_scr = nc.dram_tensor("qT_scr", [8, 128, T], BF16, kind="Internal").ap()
    kT_scr = nc.dram_tensor("kT_scr", [8, 128, T], BF16, kind="Internal").ap()
    oT_scr = nc.dram_tensor("oT_scr", [8, 128, T], BF16, kind="Internal").ap()
    v_scr = nc.dram_tensor("v_scr", [T, D], BF16, kind="Internal").ap()
    RW_NAMES = ["hs", "r", "v", "kk", "g", "kd0", "kd1", "b0", "b1", "lw0", "lw1"]
    rws = {n: nc.dram_tensor("rws_" + n, [T, D], F32, kind="Internal").ap() for n in RW_NAMES}
    D_rw = {n: Buf(rws[n]) for n in RW_NAMES}
    rws["y0"] = rws["hs"]; D_rw["y0"] = D_rw["hs"]
    rws["y1"] = rws["kd0"]; D_rw["y1"] = D_rw["kd0"]
    bon_scr = nc.dram_tensor("bon_scr", [T, 16], F32, kind="Internal").ap()
    D_bon = Buf(bon_scr)

    D_x = Buf(x_in); D_ctx = Buf(ctx_in)
    D_xb = [Buf(xbuf[0]), Buf(xbuf[1])]
    D_x1 = Buf(x1s); D_m = Buf(mscr)
    D_w1b = Buf(w1b); D_w2b = Buf(w2b); D_wqkvb = Buf(wqkvb); D_wob = Buf(wob); D_qT = Buf(qT_scr)
    D_kT = Buf(kT_scr); D_oT = Buf(oT_scr); D_v = Buf(v_scr)
    D_y = Buf(y_out); D_dbg = Buf(dbg_out) if debug else None
    D_const = Buf(None)

    with ExitStack() as st:
        P = Prog(nc)

        def sb(name, shape, dt):
            return Buf(st.enter_context(nc.sbuf_tensor(name, shape, dt)), name)

        ident16 = sb("ident16", [128, 128], BF16)
        ident32 = sb("ident32", [128, 128], F32)
        ones32 = sb("ones32", [128, 128], F32)
        ones16 = sb("ones16", [128, 128], BF16)
        condT = sb("condT", [128, 8, 2], F32)
        ARENA_WORDS = 51200
        arena_t = st.enter_context(nc.sbuf_tensor("arena", [128, ARENA_WORDS], F32))
        A = Arena(arena_t, ARENA_WORDS)
        pst = [st.enter_context(nc.psum_tensor(f"ps{k}", [128, 1024], F32)) for k in range(4)]
        bank = []
        for k in range(4):
            bank.append(Buf(pst[k][:, 0:512], f"bank{2 * k}"))
            bank.append(Buf(pst[k][:, 512:1024], f"bank{2 * k + 1}"))
        for b_ in bank:
            b_.excl = True

        def new_phase():
            P.barrier()
            A.reset()


        def eng_fn(eng):
            return {"act": P.act, "dve": P.dve, "pool": P.pool, "pe": P.pe}[eng]

        mm_state = [False]

        def MM(out, lhsT, rhs, start, stop, R, W, part=False):
            fence = part or mm_state[0]
            mm_state[0] = part
            P.pe(lambda e: e.matmul(out, lhsT=lhsT, rhs=rhs, start=start, stop=stop), R, W, fence=fence)

        def TRN(out, in_, R, W, ident=None):
            idn = ident16[:] if ident is None else ident
            P.pe(lambda e: e.transpose(out, in_, idn), list(R) + [ident16], W)

        def ACTF(out, in_, func, R, W, bias=0.0, scale=1.0, accum=None):
            if accum is None:
                P.act(lambda e: e.activation(out=out, in_=in_, func=func, bias=bias, scale=scale), R, W)
            else:
                P.act(lambda e: e.activation(out=out, in_=in_, func=func, bias=bias, scale=scale, accum_out=accum), R, W)

        def TT(eng, out, in0, in1, op, R, W):
            eng_fn(eng)(lambda e: e.tensor_tensor(out=out, in0=in0, in1=in1, op=op), R, W)

        def STT(eng, out, in0, scalar, in1, op0, op1, R, W):
            eng_fn(eng)(lambda e: e.scalar_tensor_tensor(out=out, in0=in0, scalar=scalar, in1=in1, op0=op0, op1=op1), R, W)

        def TS(eng, out, in0, s1, s2, op0, op1, R, W):
            if s2 is None:
                eng_fn(eng)(lambda e: e.tensor_scalar(out=out, in0=in0, scalar1=s1, scalar2=None, op0=op0), R, W)
            else:
                eng_fn(eng)(lambda e: e.tensor_scalar(out=out, in0=in0, scalar1=s1, scalar2=s2, op0=op0, op1=op1), R, W)

        def CPY(eng, out, in_, R, W):
            if eng == "act":
                P.act(lambda e: e.copy(out=out, in_=in_), R, W)
            else:
                eng_fn(eng)(lambda e: e.tensor_copy(out=out, in_=in_), R, W)

        def RECIP(out, in_, R, W):
            P.dve(lambda e: e.reciprocal(out=out, in_=in_), R, W)

        P.pool(lambda e: e.memset(ident32[:], 1.0), [], [ident32])
        P.pool(lambda e: e.affine_select(out=ident32[:], in_=ident32[:], pattern=[[-1, 128]], compare_op=ALU.is_equal,
                                         fill=0.0, base=0, channel_multiplier=1), [ident32], [ident32])
        P.pool(lambda e: e.tensor_copy(out=ident16[:], in_=ident32[:]), [ident32], [ident16])
        P.pool(lambda e: e.memset(ones32[:], 1.0), [], [ones32])
        P.pool(lambda e: e.memset(ones16[:], 1.0), [], [ones16])

        def src_rows(layer, r0, n):
            if layer == 0:
                if r0 < NLAT:
                    return x_in[r0:r0 + n, :], D_x
                return ctx_in[r0 - NLAT:r0 - NLAT + n, :], D_ctx
            bsel = (layer - 1) % 2
            return xbuf[bsel][r0:r0 + n, :], D_xb[bsel]

        def dst_rows(layer, r0, n):
            bsel = layer % 2
            return xbuf[bsel][r0:r0 + n, :], D_xb[bsel]

        def load_bc(buf, layer, r, j, q="sp"):
            P.dma(buf[:], mscr[layer, r:r + 1, j * D:(j + 1) * D].partition_broadcast(128), reads=[D_m], writes=[buf], q=q)

        def load_row_bc(buf, row_ap, q="sp"):
            P.dma(buf[:], row_ap.partition_broadcast(128), reads=[], writes=[buf], q=q)

        def cast_weight(src, dst, dstbuf, rows, cols, ring32, ring16, ctr):
            cb = 2048
            for r in range(rows // 128):
                for c0 in range(0, cols, cb):
                    cw = min(cb, cols - c0)
                    i = ctr[0]
                    ctr[0] += 1
                    b32 = ring32[i % len(ring32)]
                    b16 = ring16[i % len(ring16)]
                    P.dma(b32[:, 0:cw], src[r * 128:(r + 1) * 128, c0:c0 + cw], reads=[], writes=[b32], q="sp")
                    CPY("pool" if i % 2 == 0 else "act", b16[:, 0:cw], b32[:, 0:cw], [b32], [b16])
                    P.dma(dst[r * 128:(r + 1) * 128, c0:c0 + cw], b16[:, 0:cw], reads=[b16], writes=[dstbuf], q="pool")

        def layernorm(z, gbc, bbc, out, st6, mv, rs, tmp):
            P.dve(lambda e: e.bn_stats(out=st6[:, 0:6], in_=z[:, 0:512]), [z], [st6])
            P.dve(lambda e: e.bn_stats(out=st6[:, 6:12], in_=z[:, 512:1024]), [z], [st6])
            P.dve(lambda e: e.bn_aggr(out=mv[:], in_=st6[:].rearrange("p (a b) -> p a b", a=2)), [st6], [mv])
            ACTF(rs[:, 0:1], mv[:, 1:2], AF.Sqrt, [mv], [rs], bias=LN_EPS, scale=1.0)
            RECIP(rs[:, 1:2], rs[:, 0:1], [rs], [rs])
            TS("dve", tmp[:], z[:], mv[:, 0:1], rs[:, 1:2], ALU.subtract, ALU.mult, [z, mv, rs], [tmp])
            TT("pool", tmp[:], tmp[:], gbc[:], ALU.mult, [tmp, gbc], [tmp])
            TT("pool", out[:], tmp[:], bbc[:], ALU.add, [tmp, bbc], [out])

        def modulate_transpose(xt, scp, sh, h1, h2, psT, hT, t):
            TT("dve", h1[:], xt[:], scp[:], ALU.mult, [xt, scp], [h1])
            TT("pool", h2[:], h1[:], sh[:], ALU.add, [h1, sh], [h2])
            pv = psT.ap.bitcast(BF16).rearrange("p (a b) -> p a b", a=8)
            for c in range(8):
                TRN(pv[:, c, :], h2[:, c * 128:(c + 1) * 128], [h2], [psT])
            CPY("act", hT[:, :, t * 128:(t + 1) * 128], pv, [psT], [hT])

        def setup_modulation():
            new_phase()
            cc = A.alloc([128, D], F32, "cc")
            sil = A.alloc([128, D], F32, "sil")
            msb = A.alloc([128, 6 * D], F32, "msb")
            mb2 = A.alloc([128, 6 * D], F32, "mb2")
            wn = A.ring(2, [128, 8, 512], F32, "wn")
            P.dma(cc[0:2, :], cc_in[:, :], reads=[], writes=[cc])
            ACTF(sil[0:2, :], cc[0:2, :], AF.Silu, [cc], [sil])
            for c in range(8):
                MM(bank[0][:, 0:2], sil[0:2, c * 128:(c + 1) * 128], ident32[0:2, 0:2], True, True, [sil, ident32], [bank[0]])
                CPY("dve", condT[:, c, :], bank[0][:, 0:2], [bank[0]], [condT])
            for L in range(DEPTH):
                P.dma(mb2[0:2, :], din["mod_b"][L:L + 1, :].partition_broadcast(2), reads=[], writes=[mb2], q="act")
                for n in range(12):
                    w = wn[n % 2]
                    P.dma(w[:], din["mod_w"][L].rearrange("(c p) n -> p c n", p=128)[:, :, n * 512:(n + 1) * 512],
                          reads=[], writes=[w], q="sp" if n % 2 == 0 else "act")
                    pb = bank[1 + n % 2]
                    for c in range(8):
                        MM(pb[0:2, :], condT[:, c, :], w[:, c, :], c == 0, c == 7, [condT, w], [pb])
                    TT("dve", msb[0:2, n * 512:(n + 1) * 512], pb[0:2, :], mb2[0:2, n * 512:(n + 1) * 512], ALU.add, [pb, mb2], [msb])
                for j in (1, 4):
                    TS("dve", msb[0:2, j * D:(j + 1) * D], msb[0:2, j * D:(j + 1) * D], 1.0, None, ALU.add, None, [msb], [msb])
                P.dma(mscr[L, :, :], msb[0:2, :], reads=[msb], writes=[D_m], q="pool")

        def mlp_phase(L, want_ctx, final):
            new_phase()
            r32 = A.ring(3, [128, 2048], F32, "c32")
            r16 = A.ring(3, [128, 2048], BF16, "c16")
            ctr = [0]
            cast_weight(din["mlp_w1"][L], w1b, D_w1b, D, DFF, r32, r16, ctr)
            cast_weight(din["mlp_w2"][L], w2b, D_w2b, DFF, D, r32, r16, ctr)
            new_phase()
            bc = {}
            for r in range(2 if want_ctx else 1):
                for j in (3, 4, 5):
                    bc[(r, j)] = A.alloc([128, D], F32, f"bc{r}{j}")
                    load_bc(bc[(r, j)], L, r, j, q="act")
            g2 = A.alloc([128, D], F32, "g2"); b2 = A.alloc([128, D], F32, "b2")
            load_row_bc(g2, din["post_ln_g"][L, 1:2, :], q="act")
            load_row_bc(b2, din["post_ln_b"][L, 1:2, :], q="act")
            hT = A.alloc([128, 8, 512], BF16, "hT")
            uT = A.alloc([128, 32, 512], BF16, "uT")
            w1g = A.ring(2, [128, 8, 512], BF16, "w1g")
            w2g = A.ring(2, [128, 4, 512], BF16, "w2g")
            x1t = A.ring(4, [128, D], F32, "x1t")
            ysb = A.ring(4, [128, D], F32, "ysb")
            hm = A.ring(2, [128, D], F32, "hm")
            hb = A.ring(2, [128, D], BF16, "hb")
            rl = A.ring(2, [128, 512], F32, "rl")
            zt = A.ring(2, [128, D], F32, "zt")
            tmp = A.ring(2, [128, D], F32, "tmp")
            ot = A.ring(2, [128, D], F32, "ot")
            st6 = A.ring(2, [128, 12], F32, "st6"); mv = A.ring(2, [128, 2], F32, "mv"); rs = A.ring(2, [128, 2], F32, "rs")
            psT = bank[0]
            psU = [bank[1], bank[2]]
            psY = [bank[4], bank[5], bank[6], bank[7]]
            sts = [(s * 512, 512, 0) for s in range(8)]
            if want_ctx:
                sts.append((NLAT, 256, 1))
            k = 0
            wctr = 0
            for (r0, ntok, r) in sts:
                nt = ntok // 128
                scp, sh, gt = bc[(r, 4)], bc[(r, 3)], bc[(r, 5)]
                for t in range(nt):
                    xt = x1t[t]
                    P.dma(xt[:], x1s[r0 + t * 128:r0 + (t + 1) * 128, :], reads=[D_x1], writes=[xt])
                    modulate_transpose(xt, scp, sh, hm[k % 2], hb[k % 2], psT, hT, t)
                    k += 1

                def emit_U(f, wg, ntok=ntok):
                    pu = psU[f % 2]
                    for c in range(8):
                        MM(pu[:, 0:ntok], wg[:, c, (f % 4) * 128:(f % 4 + 1) * 128], hT[:, c, 0:ntok], c == 0, c == 7, [wg, hT], [pu])
                    rr = rl[f % 2]
                    ACTF(rr[:, 0:ntok], pu[:, 0:ntok], AF.Relu, [pu], [rr])
                    TT("pool", uT[:, f, 0:ntok], rr[:, 0:ntok], rr[:, 0:ntok], ALU.mult, [rr], [uT])

                def emit_Y(f, w2, first, last, nt=nt):
                    for t in range(nt):
                        MM(psY[t][:, :], uT[:, f, t * 128:(t + 1) * 128], w2[:, f % 4, :], first, last, [uT, w2], [psY[t]])

                wgs = {}
                for fg in range(8):
                    wa = w1g[wctr % 2]; wb_ = w2g[wctr % 2]; wctr += 1
                    P.dma(wa[:], w1b.rearrange("(c p) f -> p c f", p=128)[:, :, fg * 512:(fg + 1) * 512], reads=[D_w1b], writes=[wa], q="sp")
                    P.dma(wb_[:], w2b.rearrange("(f p) d -> p f d", p=128)[:, fg * 4:(fg + 1) * 4, 0:512], reads=[D_w2b], writes=[wb_], q="act")
                    wgs[fg] = wb_
                    for fi in range(4):
                        f = fg * 4 + fi
                        emit_U(f, wa)
                        if f > 0:
                            emit_Y(f - 1, wgs[(f - 1) // 4], f - 1 == 0, False)
                emit_Y(31, wgs[7], False, True)
                for t in range(nt):
                    TT("dve", ysb[t][:, 0:512], psY[t][:, :], gt[:, 0:512], ALU.mult, [psY[t], gt], [ysb[t]])
                for fg in range(8):
                    wb_ = w2g[wctr % 2]; wctr += 1
                    P.dma(wb_[:], w2b.rearrange("(f p) d -> p f d", p=128)[:, fg * 4:(fg + 1) * 4, 512:1024], reads=[D_w2b], writes=[wb_], q="act")
                    for fi in range(4):
                        f = fg * 4 + fi
                        emit_Y(f, wb_, f == 0, f == 31)
                for t in range(nt):
                    TT("dve", ysb[t][:, 512:1024], psY[t][:, :], gt[:, 512:1024], ALU.mult, [psY[t], gt], [ysb[t]])
                for t in range(nt):
                    z = zt[t % 2]; o = ot[t % 2]
                    TS("pool", z[:], x1t[t][:], ALPHA, None, ALU.mult, None, [x1t[t]], [z])
                    TT("pool", z[:], z[:], ysb[t][:], ALU.add, [z, ysb[t]], [z])
                    layernorm(z, g2, b2, o, st6[t % 2], mv[t % 2], rs[t % 2], tmp[t % 2])
                    rr0 = r0 + t * 128
                    if final:
                        if r == 0:
                            P.dma(y_out[rr0:rr0 + 128, :], o[:], reads=[o], writes=[D_y], q="pool")
                    else:
                        dap, dbuf = dst_rows(L, rr0, 128)
                        P.dma(dap, o[:], reads=[o], writes=[dbuf], q="pool")

        def mixer_epilogue(L, r0, t, psYap, psYb, gt, g1, b1, bufs, k):
            xr, yg, zt, tmp, ot, st6, mv, rs = bufs
            xt = xr[k % 2]; y_ = yg[k % 2]; z = zt[k % 2]; o = ot[k % 2]
            sap, sbuf_ = src_rows(L, r0 + t * 128, 128)
            P.dma(xt[:], sap, reads=[sbuf_], writes=[xt])
            TT("dve", y_[:], psYap[:, :], gt[:], ALU.mult, list(psYb) + [gt], [y_])
            TS("pool", z[:], xt[:], ALPHA, None, ALU.mult, None, [xt], [z])
            TT("pool", z[:], z[:], y_[:], ALU.add, [z, y_], [z])
            layernorm(z, g1, b1, o, st6[k % 2], mv[k % 2], rs[k % 2], tmp[k % 2])
            rr0 = r0 + t * 128
            P.dma(x1s[rr0:rr0 + 128, :], o[:], reads=[o], writes=[D_x1], q="pool")

        def epilogue_bufs():
            return (A.ring(2, [128, D], F32, "xr"), A.ring(2, [128, D], F32, "yg"), A.ring(2, [128, D], F32, "zt"),
                    A.ring(2, [128, D], F32, "tmp"), A.ring(2, [128, D], F32, "ot"), A.ring(2, [128, 12], F32, "st6"),
                    A.ring(2, [128, 2], F32, "mv"), A.ring(2, [128, 2], F32, "rs"))

        def gqa_phase(L, slot, want_ctx):
            new_phase()
            r32 = A.ring(3, [128, 2048], F32, "c32")
            r16 = A.ring(3, [128, 2048], BF16, "c16")
            ctr = [0]
            cast_weight(din["att_wqkv"][slot], wqkvb[:, 0:1536], D_wqkvb, D, 1536, r32, r16, ctr)
            cast_weight(din["att_wo"][slot], wob, D_wob, D, D, r32, r16, ctr)
            new_phase()
            bc = {}
            for r in range(2):
                for j in (0, 1, 2):
                    if j == 2 and r == 1 and not want_ctx:
                        continue
                    bc[(r, j)] = A.alloc([128, D], F32, f"bc{r}{j}")
                    load_bc(bc[(r, j)], L, r, j, q="act")
            g1 = A.alloc([128, D], F32, "g1"); b1 = A.alloc([128, D], F32, "b1")
            load_row_bc(g1, din["post_ln_g"][L, 0:1, :], q="act")
            load_row_bc(b1, din["post_ln_b"][L, 0:1, :], q="act")
            gq = A.alloc([128, 2], F32, "gq")
            P.dma(gq[:, 0:1], din["att_q_norm"][slot].rearrange("(p o) -> p o", o=1), reads=[], writes=[gq], q="act")
            P.dma(gq[:, 1:2], din["att_k_norm"][slot].rearrange("(p o) -> p o", o=1), reads=[], writes=[gq], q="act")
            kT = A.alloc([128, 2, T], BF16, "kT")
            vsb = A.alloc([128, NT, 256], BF16, "vsb")
            off_keep = A.off
            wq = A.alloc([128, 8, 1536], BF16, "wq")
            P.dma(wq[:], wqkvb.rearrange("(c p) n -> p c n", p=128)[:, :, 0:1536], reads=[D_wqkvb], writes=[wq])
            hT = A.alloc([128, 8, 512], BF16, "hT")
            xr = A.ring(2, [128, D], F32, "xr")
            hm = A.ring(2, [128, D], F32, "hm")
            hb = A.ring(2, [128, D], BF16, "hb")
            cs = A.ring(2, [128, 512], F32, "cs"); sn = A.ring(2, [128, 512], F32, "sn")
            sq = A.ring(2, [128, 512], F32, "sq"); rt = A.ring(2, [128, 512], F32, "rt"); rcp = A.ring(2, [128, 512], F32, "rcp")
            qn = A.ring(2, [128, 512], F32, "qn"); shf = A.ring(2, [128, 512], F32, "shf")
            t1 = A.ring(2, [128, 512], F32, "t1"); t2 = A.ring(2, [128, 512], F32, "t2")
            qo = A.ring(3, [128, 512], BF16, "qo")
            psT = bank[0]
            psQ = [bank[1], bank[2]]
            psS = [bank[3], bank[4]]
            psV = [bank[5], bank[6]]
            sts = [(s * 512, 512, 0) for s in range(8)] + [(NLAT, 256, 1)]
            k = 0; oi = 0
            for si, (r0, ntok, r) in enumerate(sts):
                nt = ntok // 128
                scp, sh = bc[(r, 1)], bc[(r, 0)]
                for t in range(nt):
                    xt = xr[k % 2]
                    sap, sbuf_ = src_rows(L, r0 + t * 128, 128)
                    P.dma(xt[:], sap, reads=[sbuf_], writes=[xt])
                    modulate_transpose(xt, scp, sh, hm[k % 2], hb[k % 2], psT, hT, t)
                    k += 1
                for t in range(nt):
                    pvv = psV[t % 2]
                    for c in range(8):
                        MM(pvv[:, 0:256], hT[:, c, t * 128:(t + 1) * 128], wq[:, c, 1280:1536], c == 0, c == 7, [hT, wq], [pvv])
                    ti = (r0 // 128) + t
                    CPY("act", vsb[:, ti, :], pvv[:, 0:256], [pvv], [vsb])
                if r == 0:
                    c_ = cs[si % 2]; s_ = sn[si % 2]
                    P.dma(c_[:], rope_cos[:, r0:r0 + 512], reads=[], writes=[c_], q="act")
                    P.dma(s_[:], rope_sin[:, r0:r0 + 512], reads=[], writes=[s_], q="act")
                for oc in range(10):
                    pq = psQ[oc % 2]; pss = psS[oc % 2]
                    sq_ = sq[oc % 2]; rt_ = rt[oc % 2]; rc_ = rcp[oc % 2]; qn_ = qn[oc % 2]
                    for c in range(8):
                        MM(pq[:, 0:ntok], wq[:, c, oc * 128:(oc + 1) * 128], hT[:, c, 0:ntok], c == 0, c == 7, [wq, hT], [pq])
                    ACTF(sq_[:, 0:ntok], pq[:, 0:ntok], AF.Square, [pq], [sq_])
                    MM(pss[:, 0:ntok], ones32[:], sq_[:, 0:ntok], True, True, [ones32, sq_], [pss])
                    ACTF(rt_[:, 0:ntok], pss[:, 0:ntok], AF.Sqrt, [pss], [rt_], bias=LN_EPS, scale=1.0 / 128)
                    RECIP(rc_[:, 0:ntok], rt_[:, 0:ntok], [rt_], [rc_])
                    gcol = gq[:, 0:1] if oc < 8 else gq[:, 1:2]
                    STT("dve", qn_[:, 0:ntok], pq[:, 0:ntok], gcol, rc_[:, 0:ntok], ALU.mult, ALU.mult, [pq, rc_, gq], [qn_])
                    if oc < 8:
                        dest = qo[oi % 3]; oi += 1
                        dap = dest[:, 0:ntok]
                    else:
                        dest = kT
                        dap = kT[:, oc - 8, r0:r0 + ntok]
                    if r == 0:
                        sh_ = shf[oc % 2]; t1_ = t1[oc % 2]; t2_ = t2[oc % 2]
                        CPY("pool", sh_[0:32, :], qn_[32:64, :], [qn_], [sh_])
                        CPY("act", sh_[32:64, :], qn_[0:32, :], [qn_], [sh_])
                        CPY("pool", sh_[64:96, :], qn_[96:128, :], [qn_], [sh_])
                        CPY("act", sh_[96:128, :], qn_[64:96, :], [qn_], [sh_])
                        TT("pool", t1_[:], qn_[:], c_[:], ALU.mult, [qn_, c_], [t1_])
                        TT("pool", t2_[:], sh_[:], s_[:], ALU.mult, [sh_, s_], [t2_])
                        TT("dve", dap, t1_[:], t2_[:], ALU.add, [t1_, t2_], [dest])
                    else:
                        CPY("pool", dap, qn_[:, 0:ntok], [qn_], [dest])
                    if oc < 8:
                        P.dma(qT_scr[oc, :, r0:r0 + ntok], dest[:, 0:ntok], reads=[dest], writes=[D_qT], q="pool")
            P.barrier()
            A.off = off_keep
            wo = A.alloc([128, 8, D], BF16, "wo")
            P.dma(wo[:], wob.rearrange("(c p) n -> p c n", p=128), reads=[D_wob], writes=[wo])
            qT = A.ring(2, [128, 8, 512], BF16, "qT")
            pT = A.ring(4, [128, 512], BF16, "pT")
            aT = A.alloc([128, 8, 512], BF16, "aT")
            rec = A.ring(2, [128, 512], F32, "rec")
            ebufs = epilogue_bufs()
            psS = [bank[0], bank[1]]
            psO = [bank[2], bank[3]]
            psR = [bank[4], bank[5]]
            psY = (bank[6], bank[7])
            psYap = pst[3]
            scale = 128.0 ** -0.5
            sts = [(s * 512, 512, 0) for s in range(8)]
            if want_ctx:
                sts.append((NLAT, 256, 1))
            pi = 0; hi = 0; k = 0
            for si, (r0, ntok, r) in enumerate(sts):
                nt = ntok // 128
                q_ = qT[si % 2]
                P.dma(q_[:, :, 0:ntok], qT_scr.rearrange("h p t -> p h t")[:, :, r0:r0 + ntok], reads=[D_qT], writes=[q_])
                kts = list(range(NT)) if r == 0 else [32, 33]
                for h in range(8):
                    kv = h // 4
                    po = psO[hi % 2]; pr = psR[hi % 2]; rc = rec[hi % 2]; hi += 1
                    for ki, kt in enumerate(kts):
                        ps_ = psS[pi % 2]; p_ = pT[pi % 4]; pi += 1
                        MM(ps_[:, 0:ntok], kT[:, kv, kt * 128:(kt + 1) * 128], q_[:, h, 0:ntok], True, True, [kT, q_], [ps_])
                        ACTF(p_[:, 0:ntok], ps_[:, 0:ntok], AF.Exp, [ps_], [p_], scale=scale)
                        first = ki == 0; last = ki == len(kts) - 1
                        MM(po[:, 0:ntok], vsb[:, kt, kv * 128:(kv + 1) * 128], p_[:, 0:ntok], first, last, [vsb, p_], [po])
                        MM(pr[:, 0:ntok], ones16[:], p_[:, 0:ntok], first, last, [ones16, p_], [pr])
                    RECIP(rc[:, 0:ntok], pr[:, 0:ntok], [pr], [rc])
                    TT("dve", aT[:, h, 0:ntok], po[:, 0:ntok], rc[:, 0:ntok], ALU.mult, [po, rc], [aT])
                gt = bc[(r, 2)]
                for t in range(nt):
                    for n in range(2):
                        for h in range(8):
                            MM(psYap[:, n * 512:(n + 1) * 512], aT[:, h, t * 128:(t + 1) * 128], wo[:, h, n * 512:(n + 1) * 512], h == 0, h == 7, [aT, wo], [psY[n]])
                    mixer_epilogue(L, r0, t, psYap, psY, gt, g1, b1, ebufs, k)
                    k += 1

        def na_phase(L, slot, want_ctx):
            new_phase()
            r32 = A.ring(3, [128, 2048], F32, "c32")
            r16 = A.ring(3, [128, 2048], BF16, "c16")
            ctr = [0]
            cast_weight(din["na_wqkv"][slot], wqkvb, D_wqkvb, D, 3072, r32, r16, ctr)
            cast_weight(din["na_wo"][slot], wob, D_wob, D, D, r32, r16, ctr)
            new_phase()
            bc = {}
            for r in range(2):
                for j in (0, 1):
                    bc[(r, j)] = A.alloc([128, D], F32, f"bc{r}{j}")
                    load_bc(bc[(r, j)], L, r, j, q="act")
            wq = A.alloc([128, 8, 3072], BF16, "wq")
            P.dma(wq[:], wqkvb.rearrange("(c p) n -> p c n", p=128), reads=[D_wqkvb], writes=[wq])
            hT = A.alloc([128, 8, 512], BF16, "hT")
            xr = A.ring(2, [128, D], F32, "xr")
            hm = A.ring(2, [128, D], F32, "hm")
            hb = A.ring(2, [128, D], BF16, "hb")
            qo = A.ring(4, [128, 512], BF16, "qo")
            vo = A.ring(3, [128, D], BF16, "vo")
            psT = bank[0]
            psQ = [bank[1], bank[2], bank[3]]
            psVt = [pst[2], pst[3]]
            psVb = [(bank[4], bank[5]), (bank[6], bank[7])]
            sts = [(s * 512, 512, 0) for s in range(8)] + [(NLAT, 256, 1)]
            k = 0; oi = 0
            for si, (r0, ntok, r) in enumerate(sts):
                nt = ntok // 128
                scp, sh = bc[(r, 1)], bc[(r, 0)]
                for t in range(nt):
                    xt = xr[k % 2]
                    sap, sbuf_ = src_rows(L, r0 + t * 128, 128)
                    P.dma(xt[:], sap, reads=[sbuf_], writes=[xt])
                    modulate_transpose(xt, scp, sh, hm[k % 2], hb[k % 2], psT, hT, t)
                    k += 1
                for oc in range(16):
                    pq = psQ[oc % 3]
                    for c in range(8):
                        MM(pq[:, 0:ntok], wq[:, c, oc * 128:(oc + 1) * 128], hT[:, c, 0:ntok], c == 0, c == 7, [wq, hT], [pq])
                    dest = qo[oi % 4]; oi += 1
                    if oc < 8:
                        ACTF(dest[:, 0:ntok], pq[:, 0:ntok], AF.Copy, [pq], [dest], scale=0.125)
                        P.dma(qT_scr[oc, :, r0:r0 + ntok], dest[:, 0:ntok], reads=[dest], writes=[D_qT], q="pool")
                    else:
                        CPY("dve", dest[:, 0:ntok], pq[:, 0:ntok], [pq], [dest])
                        P.dma(kT_scr[oc - 8, :, r0:r0 + ntok], dest[:, 0:ntok], reads=[dest], writes=[D_kT], q="pool")
                for t in range(nt):
                    pvt = psVt[t % 2]; pvb = psVb[t % 2]
                    for n in range(2):
                        for c in range(8):
                            MM(pvt[:, n * 512:(n + 1) * 512], hT[:, c, t * 128:(t + 1) * 128], wq[:, c, 2048 + n * 512:2048 + (n + 1) * 512], c == 0, c == 7, [hT, wq], [pvb[n]])
                    v_ = vo[t % 3]
                    CPY("act" if t % 2 == 0 else "dve", v_[:], pvt[:, :], list(pvb), [v_])
                    P.dma(v_scr[r0 + t * 128:r0 + (t + 1) * 128, :], v_[:], reads=[v_], writes=[D_v], q="pool")
            new_phase()
            qTm = A.ring(2, [128, T], BF16, "qTm")
            kTm = A.ring(2, [128, T], BF16, "kTm")
            vm = A.ring(2, [128, NT, 128], BF16, "vm")
            oTm = A.ring(2, [128, T], BF16, "oTm")
            btab = A.ring(2, [128, 10, 640], F32, "btab")
            ssb = A.ring(2, [128, 896], F32, "ssb")
            pex = A.ring(2, [128, 896], F32, "pex")
            pn = A.ring(2, [128, 896], BF16, "pn")
            pTs = A.ring(2, [128, 7, 128], BF16, "pTs")
            sm = A.ring(4, [128, 4], F32, "sm")
            psS = [pst[0], pst[1]]
            psSb = [(bank[0], bank[1]), (bank[2], bank[3])]
            psTr = [bank[4], bank[5]]
            psO = [bank[6], bank[7]]
            ui = 0

            def attend(m, a, q0, segs, bias_ap, btab_buf, vtiles):
                nonlocal ui
                u = ui; ui += 1
                ps = psS[u % 2]; psb = psSb[u % 2]
                q_ = qTm[m % 2]; k_ = kTm[m % 2]; v_ = vm[m % 2]; o_ = oTm[m % 2]
                s_ = ssb[u % 2]; pe_ = pex[u % 2]; pn_ = pn[u % 2]; pt_ = pTs[u % 2]; sm_ = sm[u % 4]
                lo = a * 64; hi = lo + 64
                col = 0
                for (k0, nk) in segs:
                    bsel = psb[0] if col < 512 else psb[1]
                    MM(ps[:, col:col + nk], q_[lo:hi, q0:q0 + 128], k_[lo:hi, k0:k0 + nk], True, True, [q_, k_], [bsel])
                    col += nk
                ntot = col
                nb = 640 if bias_ap is not None else 0
                if nb:
                    TT("dve", s_[:, 0:nb], ps[:, 0:nb], bias_ap, ALU.add, [psb[0], psb[1], btab_buf], [s_])
                CPY("act", s_[:, nb:ntot], ps[:, nb:ntot], list(psb), [s_])
                P.dve(lambda e: e.reduce_max(out=sm_[:, 0:1], in_=s_[:, 0:ntot], axis=AX.X), [s_], [sm_])
                TS("dve", sm_[:, 1:2], sm_[:, 0:1], -1.0, None, ALU.mult, None, [sm_], [sm_])
                ACTF(pe_[:, 0:ntot], s_[:, 0:ntot], AF.Exp, [s_, sm_], [pe_, sm_], bias=sm_[:, 1:2], scale=1.0, accum=sm_[:, 2:3])
                RECIP(sm_[:, 3:4], sm_[:, 2:3], [sm_], [sm_])
                TS("pool", pn_[:, 0:ntot], pe_[:, 0:ntot], sm_[:, 3:4], None, ALU.mult, None, [pe_, sm_], [pn_])
                nkt = ntot // 128
                ptr = psTr[u % 2]
                pv = ptr.ap.bitcast(BF16).rearrange("p (a b) -> p a b", a=8)
                for kt in range(nkt):
                    TRN(pv[:, kt, :], pn_[:, kt * 128:(kt + 1) * 128], [pn_], [ptr])
                CPY("act", pt_[:, 0:nkt, :], pv[:, 0:nkt, :], [ptr], [pt_])
                po = psO[u % 2]
                for kt in range(nkt):
                    MM(po[:, 0:128], v_[:, vtiles[kt], :], pt_[:, kt, :], kt == 0, kt == nkt - 1, [v_, pt_], [po])
                CPY("dve", o_[lo:hi, q0:q0 + 128], po[lo:hi, 0:128], [po], [o_])

            for m in range(8):
                q_ = qTm[m % 2]; k_ = kTm[m % 2]; v_ = vm[m % 2]; o_ = oTm[m % 2]; bt = btab[m % 2]
                P.dma(q_[:], qT_scr[m], reads=[D_qT], writes=[q_])
                P.dma(k_[:], kT_scr[m], reads=[D_kT], writes=[k_])
                for t0 in range(0, NT, 6):
                    t1_ = min(NT, t0 + 6)
                    P.dma(v_[:, t0:t1_, :], v_scr.rearrange("(t p) d -> p t d", p=128)[:, t0:t1_, m * 128:(m + 1) * 128], reads=[D_v], writes=[v_], q="act")
                for a in range(2):
                    P.dma(bt[:, a * 5:(a + 1) * 5, :], na_bias[2 * m + a].rearrange("t p k -> p t k"), reads=[], writes=[bt], q="act")
                for p in range(32):
                    ws = min(max(2 * p - 4, 0), 54)
                    typ = 0 if p == 0 else 1 if p == 1 else 3 if p == 30 else 4 if p == 31 else 2
                    segs = [(ws * 64, 512), (ws * 64 + 512, 128), (NLAT, 256)]
                    vt = [ws // 2 + i for i in range(5)] + [32, 33]
                    for a in range(2):
                        attend(m, a, p * 128, segs, bt[:, a * 5 + typ, :], bt, vt)
                if want_ctx:
                    for ct in range(2):
                        for a in range(2):
                            attend(m, a, NLAT + ct * 128, [(NLAT, 256)], None, bt, [32, 33])
                    P.dma(oT_scr[m], o_[:], reads=[o_], writes=[D_oT], q="pool")
                else:
                    P.dma(oT_scr[m, :, 0:NLAT], o_[:, 0:NLAT], reads=[o_], writes=[D_oT], q="pool")
            out_proj_phase(L, want_ctx)

        def out_proj_phase(L, want_ctx):
            new_phase()
            bc = {}
            for r in range(2 if want_ctx else 1):
                bc[r] = A.alloc([128, D], F32, f"bcg{r}")
                load_bc(bc[r], L, r, 2, q="act")
            g1 = A.alloc([128, D], F32, "g1"); b1 = A.alloc([128, D], F32, "b1")
            load_row_bc(g1, din["post_ln_g"][L, 0:1, :], q="act")
            load_row_bc(b1, din["post_ln_b"][L, 0:1, :], q="act")
            wo = A.alloc([128, 8, D], BF16, "wo")
            P.dma(wo[:], wob.rearrange("(c p) n -> p c n", p=128), reads=[D_wob], writes=[wo])
            oT = A.ring(2, [128, 8, 512], BF16, "oT")
            ebufs = epilogue_bufs()
            psYs = [(pst[2], (bank[4], bank[5])), (pst[3], (bank[6], bank[7]))]
            sts = [(s * 512, 512, 0) for s in range(8)]
            if want_ctx:
                sts.append((NLAT, 256, 1))
            k = 0
            for si, (r0, ntok, r) in enumerate(sts):
                nt = ntok // 128
                o_ = oT[si % 2]
                P.dma(o_[:, :, 0:ntok], oT_scr.rearrange("m p t -> p m t")[:, :, r0:r0 + ntok], reads=[D_oT], writes=[o_])
                for t in range(nt):
                    psYap, psYb = psYs[k % 2]
                    for n in range(2):
                        for m in range(8):
                            MM(psYap[:, n * 512:(n + 1) * 512], o_[:, m, t * 128:(t + 1) * 128], wo[:, m, n * 512:(n + 1) * 512], m == 0, m == 7, [o_, wo], [psYb[n]])
                    mixer_epilogue(L, r0, t, psYap, psYb, bc[r], g1, b1, ebufs, k)
                    k += 1

        def rwkv_phase(L, slot, want_ctx):
            pw = {n: din[n][slot] for n in W_NAMES if n.startswith("rw_")}
            new_phase()
            r32 = A.ring(3, [128, 2048], F32, "c32")
            r16 = A.ring(3, [128, 2048], BF16, "c16")
            ctr = [0]
            cast_weight(pw["rw_wr"], wqkvb[:, 0:1024], D_wqkvb, D, D, r32, r16, ctr)
            cast_weight(pw["rw_wk"], wqkvb[:, 1024:2048], D_wqkvb, D, D, r32, r16, ctr)
            cast_weight(pw["rw_wv"], wqkvb[:, 2048:3072], D_wqkvb, D, D, r32, r16, ctr)
            cast_weight(pw["rw_wo"], wob, D_wob, D, D, r32, r16, ctr)
            new_phase()
            bc = {}
            for r in range(2):
                for j in (0, 1):
                    bc[(r, j)] = A.alloc([128, D], F32, f"bc{r}{j}")
                    load_bc(bc[(r, j)], L, r, j, q="act")
            xr = A.ring(3, [128, D], F32, "xr"); hm = A.ring(3, [128, D], F32, "hm")
            for t in range(NT):
                r = 0 if t < 32 else 1
                xt = xr[t % 3]; h_ = hm[t % 3]
                sap, sbuf_ = src_rows(L, t * 128, 128)
                P.dma(xt[:], sap, reads=[sbuf_], writes=[xt])
                TT("dve", h_[:], xt[:], bc[(r, 1)][:], ALU.mult, [xt, bc[(r, 1)]], [h_])
                TT("pool", h_[:], h_[:], bc[(r, 0)][:], ALU.add, [h_, bc[(r, 0)]], [h_])
                P.dma(rws["hs"][t * 128:(t + 1) * 128, :], h_[:], reads=[h_], writes=[D_rw["hs"]], q="pool")
            if rw_stop < 1:
                return
            new_phase()
            wrkv = A.alloc([128, 8, 3072], BF16, "wrkv")
            P.dma(wrkv[:], wqkvb.rearrange("(c p) n -> p c n", p=128), reads=[D_wqkvb], writes=[wrkv])
            xx = A.alloc([128, D], F32, "xx")
            stg = Buf(xx.ap.rearrange("p (n c r) -> p n c r", n=2, c=8), "stg"); stg = xx.__class__(stg.ap, "stg") if False else xx
            stg_ap = xx.ap.rearrange("p (n c r) -> p n c r", n=2, c=8)
            w1b_ = A.alloc([128, 2, 8, 64], BF16, "w1b_"); a1b_ = A.alloc([128, 2, 8, 64], BF16, "a1b_")
            for (nm, dst) in (("rw_w1", w1b_), ("rw_a1", a1b_)):
                for n in range(2):
                    P.dma(stg_ap[:, n, :, :], pw[nm][n].rearrange("(c p) r -> p c r", p=128), reads=[], writes=[xx])
                CPY("pool", dst[:], stg_ap, [xx], [dst])
            stg2 = A.alloc([128, 2, D], F32, "hpn")
            w2b_ = A.alloc([128, 2, D], BF16, "w2b_"); a2b_ = A.alloc([128, 2, D], BF16, "a2b_")
            for (nm, dst) in (("rw_w2", w2b_), ("rw_a2", a2b_)):
                P.dma(stg2[0:64, :, :], pw[nm].rearrange("n r d -> r n d"), reads=[], writes=[stg2])
                CPY("pool", dst[0:64], stg2[0:64], [stg2], [dst])
            g1b_ = A.alloc([128, 8, 128], BF16, "g1b_"); g2b_ = A.alloc([128, D], BF16, "g2b_")
            P.dma(stg2[:, 0, :].rearrange("p (c r) -> p c r", c=8), pw["rw_g1"].rearrange("(c p) r -> p c r", p=128), reads=[], writes=[stg2])
            CPY("pool", g1b_[:], stg2[:, 0, :].rearrange("p (c r) -> p c r", c=8), [stg2], [g1b_])
            P.dma(stg2[:, 1, :], pw["rw_g2"], reads=[], writes=[stg2])
            CPY("pool", g2b_[:], stg2[:, 1, :], [stg2], [g2b_])
            mu = A.ring(6, [128, D], F32, "mu")
            for j in range(6):
                load_row_bc(mu[j], pw["rw_mu"][j:j + 1, :], q="act")
            vb = {}
            for nm, ap_ in (("w00", pw["rw_w0"][0:1, :]), ("w01", pw["rw_w0"][1:2, :]), ("a00", pw["rw_a0"][0:1, :]), ("a01", pw["rw_a0"][1:2, :]),
                            ("kkb", pw["rw_k_k"].rearrange("(o d) -> o d", o=1)), ("kab", pw["rw_k_a"].rearrange("(o d) -> o d", o=1)),
                            ("rkb", pw["rw_r_k"].rearrange("(o h) n -> o (h n)", o=1))):
                vb[nm] = A.alloc([128, D], F32, nm)
                load_row_bc(vb[nm], ap_, q="act")
            omka = A.alloc([128, D], F32, "omka")
            TS("dve", omka[:], vb["kab"][:], -1.0, 1.0, ALU.mult, ALU.add, [vb["kab"]], [omka])
            hc1 = A.alloc([128, D], F32, "hc")
            xm = A.ring(2, [128, D], F32, "xm"); xj = A.ring(2, [128, D], BF16, "xj")
            xT = A.ring(6, [128, 8, 128], BF16, "xT")
            rsb = A.alloc([128, D], F32, "rsb"); ksb = A.alloc([128, D], F32, "ksb")
            kks = A.alloc([128, D], F32, "kks"); rbs = A.alloc([128, D], F32, "rbs")
            t1s = A.alloc([128, D], F32, "t1"); t1 = [t1s, t1s]; t2 = A.ring(2, [128, D], F32, "t2"); t3 = A.ring(2, [128, D], F32, "t3")
            gsb = t2[0]; vsb_ = t3[1]
            lth = A.ring(2, [128, 128], BF16, "lth")
            s16 = A.ring(2, [128, 64], F32, "s16")
            psT = bank[0]; psl = [bank[1]]
            big = [(pst[1], (bank[2], bank[3])), (pst[2], (bank[4], bank[5])), (pst[3], (bank[6], bank[7]))]
            bi = 0
            for t in range(NT):
                k = t % 2
                r0 = t * 128
                first = (t == 0 or t == 32); last = (t == 31 or t == 33)
                c_ = hc1
                pA = stg2[:, 0, :]; nA = stg2[:, 1, :]
                P.dma(c_[:], rws["hs"][r0:r0 + 128, :], reads=[D_rw["hs"]], writes=[c_])
                if first:
                    P.pool(lambda e: e.memset(stg2[0:1, 0, :], 0.0), [], [stg2])
                    P.dma(stg2[1:128, 0, :], rws["hs"][r0:r0 + 127, :], reads=[D_rw["hs"]], writes=[stg2])
                else:
                    P.dma(pA, rws["hs"][r0 - 1:r0 + 127, :], reads=[D_rw["hs"]], writes=[stg2])
                if last:
                    P.pool(lambda e: e.memset(stg2[96:128, 1, :], 0.0), [], [stg2])
                    P.dma(stg2[0:127, 1, :], rws["hs"][r0 + 1:r0 + 128, :], reads=[D_rw["hs"]], writes=[stg2], q="act")
                else:
                    P.dma(nA, rws["hs"][r0 + 1:r0 + 129, :], reads=[D_rw["hs"]], writes=[stg2], q="act")
                TT("pool", pA, pA, nA, ALU.add, [stg2], [stg2])
                STT("dve", xx[:], pA, 0.5, c_[:], ALU.mult, ALU.subtract, [stg2, c_], [xx])
                for j in range(6):
                    m_ = xm[j % 2]; x_ = xj[j % 2]
                    TT("pool", m_[:], xx[:], mu[j][:], ALU.mult, [xx, mu[j]], [m_])
                    TT("dve" if j % 2 else "pool", x_[:], m_[:], c_[:], ALU.add, [m_, c_], [x_])
                    pv = psT.ap.bitcast(BF16).rearrange("p (a b) -> p a b", a=8)
                    for c in range(8):
                        TRN(pv[:, c, :], x_[:, c * 128:(c + 1) * 128], [x_], [psT])
                    CPY("act", xT[j][:], pv, [psT], [xT[j]])

                def proj(j, col0, dst):
                    nonlocal bi
                    pa, pb = big[bi % 3]; bi += 1
                    for n in range(2):
                        for c in range(8):
                            MM(pa[:, n * 512:(n + 1) * 512], xT[j][:, c, :], wrkv[:, c, col0 + n * 512:col0 + (n + 1) * 512], c == 0, c == 7, [xT[j], wrkv], [pb[n]])
                    CPY("act", dst[:], pa[:, :], list(pb), [dst])

                proj(0, 0, rsb); proj(2, 1024, ksb); proj(3, 2048, vsb_)
                P.dma(rws["r"][r0:r0 + 128, :], rsb[:], reads=[rsb], writes=[D_rw["r"]], q="pool")
                P.dma(rws["v"][r0:r0 + 128, :], vsb_[:], reads=[vsb_], writes=[D_rw["v"]], q="pool")
                pg = psl[0]
                for c in range(8):
                    MM(pg[:, 0:128], g1b_[:, c, :], xT[5][:, c, :], c == 0, c == 7, [g1b_, xT[5]], [pg])
                lg = lth[0]
                ACTF(lg[:], pg[:, 0:128], AF.Sigmoid, [pg], [lg])
                pa, pb = big[bi % 3]; bi += 1
                for n in range(2):
                    MM(pa[:, n * 512:(n + 1) * 512], lg[:], g2b_[:, n * 512:(n + 1) * 512], True, True, [lg, g2b_], [pb[n]])
                CPY("act", gsb[:], pa[:, :], list(pb), [gsb])
                P.dma(rws["g"][r0:r0 + 128, :], gsb[:], reads=[gsb], writes=[D_rw["g"]], q="pool")
                TT("pool", kks[:], ksb[:], vb["kkb"][:], ALU.mult, [ksb, vb["kkb"]], [kks])
                q1 = t1[0]
                TT("pool", q1[:], kks[:], kks[:], ALU.mult, [kks], [q1])
                sv = s16[t % 2]
                P.dve(lambda e, sv=sv, q1=q1: e.tensor_reduce(out=sv[:, 0:16], in_=q1[:].rearrange("p (h n) -> p h n", h=16), axis=AX.X, op=ALU.add), [q1], [sv])
                TS("dve", sv[:, 0:16], sv[:, 0:16], 1e-12, None, ALU.max, None, [sv], [sv])
                ACTF(sv[:, 16:32], sv[:, 0:16], AF.Sqrt, [sv], [sv])
                RECIP(sv[:, 32:48], sv[:, 16:32], [sv], [sv])
                for hd in range(16):
                    TS("pool", kks[:, hd * 64:(hd + 1) * 64], kks[:, hd * 64:(hd + 1) * 64], sv[:, 32 + hd:33 + hd], None, ALU.mult, None, [kks, sv], [kks])
                P.dma(rws["kk"][r0:r0 + 128, :], kks[:], reads=[kks], writes=[D_rw["kk"]], q="pool")
                TT("pool", rbs[:], rsb[:], vb["rkb"][:], ALU.mult, [rsb, vb["rkb"]], [rbs])
                for d in range(2):
                    pl = psl[0]
                    for c in range(8):
                        MM(pl[0:64, 0:128], w1b_[:, d, c, :], xT[1][:, c, :], c == 0, c == 7, [w1b_, xT[1]], [pl])
                    lt = lth[1]
                    ACTF(lt[0:64, :], pl[0:64, 0:128], AF.Tanh, [pl], [lt])
                    pa, pb = big[bi % 3]; bi += 1
                    for n in range(2):
                        MM(pa[:, n * 512:(n + 1) * 512], lt[0:64, :], w2b_[0:64, d, n * 512:(n + 1) * 512], True, True, [lt, w2b_], [pb[n]])
                    u1 = t1[1]; u2 = t2[d]
                    TT("dve", u1[:], pa[:, :], vb[f"w0{d}"][:], ALU.add, list(pb) + [vb[f"w0{d}"]], [u1])
                    ACTF(u1[:], u1[:], AF.Exp, [u1], [u1], scale=-1.0)
                    ACTF(u1[:], u1[:], AF.Ln, [u1], [u1], bias=1.0)
                    ACTF(u1[:], u1[:], AF.Exp, [u1], [u1], scale=-1.0, bias=-0.5)
                    TS("pool", u2[:], u1[:], -1.0, None, ALU.mult, None, [u1], [u2])
                    P.dma(rws[f"lw{d}"][r0:r0 + 128, :], u2[:], reads=[u2], writes=[D_rw[f"lw{d}"]], q="pool")
                    pl = psl[0]
                    for c in range(8):
                        MM(pl[0:64, 0:128], a1b_[:, d, c, :], xT[4][:, c, :], c == 0, c == 7, [a1b_, xT[4]], [pl])
                    lt = lth[1]
                    CPY("act", lt[0:64, :], pl[0:64, 0:128], [pl], [lt])
                    pa, pb = big[bi % 3]; bi += 1
                    for n in range(2):
                        MM(pa[:, n * 512:(n + 1) * 512], lt[0:64, :], a2b_[0:64, d, n * 512:(n + 1) * 512], True, True, [lt, a2b_], [pb[n]])
                    a_ = t3[0]; b_ = t3[1]
                    TT("dve", a_[:], pa[:, :], vb[f"a0{d}"][:], ALU.add, list(pb) + [vb[f"a0{d}"]], [a_])
                    ACTF(a_[:], a_[:], AF.Sigmoid, [a_], [a_])
                    TT("pool", b_[:], kks[:], a_[:], ALU.mult, [kks, a_], [b_])
                    P.dma(rws[f"b{d}"][r0:r0 + 128, :], b_[:], reads=[b_], writes=[D_rw[f"b{d}"]], q="pool")
                    TT("pool", a_[:], a_[:], vb["kab"][:], ALU.mult, [a_, vb["kab"]], [a_])
                    TT("pool", a_[:], a_[:], omka[:], ALU.add, [a_, omka], [a_])
                    kd_ = xm[d]
                    TT("pool", kd_[:], ksb[:], a_[:], ALU.mult, [ksb, a_], [kd_])
                    P.dma(rws[f"kd{d}"][r0:r0 + 128, :], kd_[:], reads=[kd_], writes=[D_rw[f"kd{d}"]], q="pool")
                    TT("pool", a_[:], rbs[:], kd_[:], ALU.mult, [rbs, kd_], [a_])
                    so = sv[:, 0:16] if d == 0 else sv[:, 16:32]
                    P.dve(lambda e, so=so, a_=a_: e.tensor_reduce(out=so, in_=a_[:].rearrange("p (h n) -> p h n", h=16), axis=AX.X, op=ALU.add), [a_], [sv])
                TT("dve", sv[:, 32:48], sv[:, 0:16], sv[:, 16:32], ALU.add, [sv], [sv])
                P.dma(bon_scr[r0:r0 + 128, :], sv[:, 32:48], reads=[sv], writes=[D_bon], q="pool")
            if rw_stop < 2:
                return
            for d in range(2):
                if rw_stop < 3 and d == 1:
                    return
                new_phase()
                rwc = A.alloc([128, 1282], F32, "rwc")
                P.dma(rwc[:], rwc_in[:, :], reads=[], writes=[rwc])
                o_ = d * 640
                TRI = rwc[:, o_:o_ + 128]; TRI2 = rwc[:, o_ + 128:o_ + 256]; MSI = rwc[:, o_ + 256:o_ + 512]; MLT = rwc[:, o_ + 512:o_ + 640]
                CH = rwc[:, 1280:1282]
                m4m = A.alloc([128, 512], F32, "m4m")
                CPY("pool", m4m[:, 0:256], MSI, [rwc], [m4m])
                CPY("pool", m4m[:, 256:512], MSI, [rwc], [m4m])
                names = ["r", "v", "kk", f"kd{d}", f"b{d}", f"lw{d}"]
                ld = {n: A.ring(2, [128, D], F32, "ld" + n) for n in names}
                ecum = A.alloc([128, D], F32, "ecum"); encum = A.alloc([128, D], F32, "encum"); eE = A.alloc([128, D], F32, "eE"); ecm = A.alloc([128, D], F32, "ecm")
                At = A.alloc([128, D], F32, "At"); Rt = A.alloc([128, D], F32, "Rt")
                Kt16 = A.alloc([128, D], BF16, "Kt16"); Bt16 = A.alloc([128, D], BF16, "Bt16")
                Khr = A.ring(2, [128, D], F32, "Kh"); Bhr = A.ring(2, [128, D], F32, "Bh")
                Vpr = A.ring(2, [128, 2, D], F32, "Vp")
                FKr = A.ring(2, [128, 8, 128], BF16, "FK"); FBr = A.ring(2, [128, 8, 128], BF16, "FB")
                AR16r = [A.ring(2, [128, 8, 256], BF16, f"AR16{p}") for p in range(2)]
                AR32r = [A.ring(1, [128, 8, 256], F32, f"AR32{p}") for p in range(2)]
                gamr = A.ring(2, [128, 16], F32, "gam")
                Hp = A.alloc([128, 16, 64], F32, "Hp")
                M4 = A.ring(4, [128, 512], F32, "M4")
                L16 = A.ring(8, [128, 128], BF16, "L16"); P16 = A.ring(8, [128, 128], BF16, "P16"); Q16 = A.ring(8, [128, 128], BF16, "Q16")
                Q32 = A.ring(4, [128, 128], F32, "Q32"); LMVr = A.ring(4, [128, 128], F32, "LMV")
                Wp = A.ring(8, [128, 64], F32, "Wp"); Up = A.ring(8, [128, 64], F32, "Up"); tYr = A.ring(8, [128, 64], F32, "tY")
                Yt = A.ring(2, [128, D], F32, "Yt")
                for z in [Hp] + Vpr + AR16r[0] + AR16r[1] + AR32r[0] + AR32r[1] + Wp + Up:
                    P.pool(lambda e, z=z: e.memset(z[:], 0.0), [], [z])
                HSb = [Buf(Hp.ap[:, h, :], f"HS{h}") for h in range(16)]
                for hb_ in HSb:
                    hb_.w = Hp.w
                order = [32, 33] + list(range(32)) if d == 0 else [33, 32] + list(range(31, -1, -1))
                halves = (0, 1) if d == 0 else (1, 0)
                for ti, t in enumerate(order):
                    r0 = t * 128
                    cur = {}
                    for qi, n in enumerate(names):
                        b_ = ld[n][ti % 2]
                        P.dma(b_[:], rws[n][r0:r0 + 128, :], reads=[D_rw[n]], writes=[b_], q="sp" if qi % 2 == 0 else "act")
                        cur[n] = b_
                    r_, v_, kk_, kd_, bb_, lw_ = [cur[n] for n in names]
                    Kh = Khr[ti % 2]; Bh = Bhr[ti % 2]; Vp = Vpr[ti % 2]; FK = FKr[ti % 2]; FB = FBr[ti % 2]; gam = gamr[ti % 2]
                    AR16 = [AR16r[0][ti % 2], AR16r[1][ti % 2]]; AR32 = [AR32r[0][0], AR32r[1][0]]
                    for n in range(2):
                        MM(pst[0][:, n * 512:(n + 1) * 512], TRI, lw_[:, n * 512:(n + 1) * 512], True, True, [rwc, lw_], [bank[n]])
                    for n in range(2):
                        MM(pst[1][:, n * 512:(n + 1) * 512], TRI2, lw_[:, n * 512:(n + 1) * 512], True, True, [rwc, lw_], [bank[2 + n]])
                    for m in range(8):
                        MM(bank[5][:, 2 * m:2 * m + 2], lw_[:, m * 128:(m + 1) * 128], CH, True, True, [lw_, rwc], [bank[5]])
                    pb = [bank[0], bank[1]]; pb2 = [bank[2], bank[3]]
                    ACTF(ecum[:], pst[0][:, :], AF.Exp, pb, [ecum])
                    ACTF(encum[:], pst[0][:, :], AF.Exp, pb, [encum], scale=-1.0)
                    TT("dve", ecm[:], pst[0][:, :], lw_[:], ALU.subtract, pb + [lw_], [ecm])
                    ACTF(eE[:], pst[1][:, :], AF.Exp, pb2, [eE])
                    ACTF(ecm[:], ecm[:], AF.Exp, [ecm], [ecm])
                    ACTF(gam[:], bank[5][:, 0:16], AF.Exp, [bank[5]], [gam])
                    TT("pool", Rt[:], r_[:], ecum[:], ALU.mult, [r_, ecum], [Rt])
                    STT("dve", At[:], kk_[:], -1.0, ecm[:], ALU.mult, ALU.mult, [kk_, ecm], [At])
                    TT("pool", Kt16[:], kd_[:], encum[:], ALU.mult, [kd_, encum], [Kt16])
                    TT("pool", Bt16[:], bb_[:], encum[:], ALU.mult, [bb_, encum], [Bt16])
                    TT("pool", Kh[:], kd_[:], eE[:], ALU.mult, [kd_, eE], [Kh])
                    TT("dve", Bh[:], bb_[:], eE[:], ALU.mult, [bb_, eE], [Bh])
                    CPY("pool", Vp[0:64, 0, :], v_[0:64, :], [v_], [Vp])
                    CPY("pool", Vp[64:128, 1, :], v_[64:128, :], [v_], [Vp])
                    for (src, dstF, bk) in ((Kt16, FK, bank[6]), (Bt16, FB, bank[7])):
                        pv = bk.ap.bitcast(BF16).rearrange("p (a b) -> p a b", a=8)
                        for m in range(8):
                            TRN(pv[:, m, :], src[:, m * 128:(m + 1) * 128], [src], [bk])
                        CPY("act", dstF[:], pv, [bk], [dstF])
                    for mp in range(4):
                        bk = bank[4 + mp % 2]
                        for mi in range(2):
                            m = 2 * mp + mi
                            for ki, src in enumerate((At, Rt)):
                                c0 = mi * 256 + ki * 128
                                P.pe(lambda e, bk=bk, c0=c0, src=src, m=m: e.transpose(bk[:, c0:c0 + 128], src[:, m * 128:(m + 1) * 128], ident32[:]), [src, ident32], [bk])
                        for par in range(2):
                            pr = slice(par * 64, par * 64 + 64)
                            CPY("act", AR32[par][pr, 2 * mp:2 * mp + 2, :], bk[pr, :].rearrange("p (a b) -> p a b", a=2), [bk], [AR32[par]])
                            CPY("dve", AR16[par][pr, 2 * mp:2 * mp + 2, :], bk[pr, :].rearrange("p (a b) -> p a b", a=2), [bk], [AR16[par]])
                    Y_ = Yt[ti % 2]
                    for g in range(4):
                        hs = [4 * g + i for i in range(4)]

                        def reg(i, k):
                            idx = i * 3 + k
                            return bank[5 + idx // 4], (idx % 4) * 128

                        Mh = [M4[i] for i in range(4)]
                        cur_L = [L16[2 * i] for i in range(4)]; cur_P = [P16[2 * i] for i in range(4)]; cur_Q = [Q16[2 * i] for i in range(4)]
                        for i, h in enumerate(hs):
                            m = h // 2; par = h % 2
                            pq = bank[i]
                            MM(pq[:, 0:256], FB[:, m, :], AR16[par][:, m, :], True, True, [FB, AR16[par]], [pq])
                            MM(pq[:, 256:512], FK[:, m, :], AR16[par][:, m, :], True, True, [FK, AR16[par]], [pq])
                            MM(bank[4][:, i * 128:(i + 1) * 128], AR16[par][:, m, 0:128], FB[:, m, :], True, True, [FB, AR16[par]], [bank[4]])
                        for i, h in enumerate(hs):
                            TT("dve", Mh[i][:], bank[i][:, :], m4m[:], ALU.mult, [bank[i], m4m], [Mh[i]])
                            TT("dve", cur_L[i][:], bank[4][:, i * 128:(i + 1) * 128], MLT, ALU.mult, [bank[4], rwc], [cur_L[i]])
                            CPY("act", cur_P[i][:], Mh[i][:, 0:128], [Mh[i]], [cur_P[i]])
                            TT("pool", cur_Q[i][:], Mh[i][:, 0:128], ident32[:], ALU.add, [Mh[i], ident32], [cur_Q[i]])
                        for lev in range(1, 6):
                            nL = [L16[2 * i + lev % 2] for i in range(4)]; nP = [P16[2 * i + lev % 2] for i in range(4)]
                            nQ = [Q16[2 * i + lev % 2] for i in range(4)] if lev < 5 else [Q32[i] for i in range(4)]
                            for i in range(4):
                                b1, c1 = reg(i, 1)
                                MM(b1[:, c1:c1 + 128], cur_P[i][:], cur_L[i][:], True, True, [cur_L[i], cur_P[i]], [b1])
                                if lev < 5:
                                    b0, c0 = reg(i, 0)
                                    MM(b0[:, c0:c0 + 128], cur_L[i][:], cur_P[i][:], True, True, [cur_L[i], cur_P[i]], [b0])
                            for i in range(4):
                                b1, c1 = reg(i, 1)
                                CPY("act", nL[i][:], b1[:, c1:c1 + 128], [b1], [nL[i]])
                                if lev < 5:
                                    b0, c0 = reg(i, 0)
                                    CPY("dve", nP[i][:], b0[:, c0:c0 + 128], [b0], [nP[i]])
                            for i in range(4):
                                b2, c2 = reg(i, 2)
                                MM(b2[:, c2:c2 + 128], nL[i][:], cur_Q[i][:], True, True, [nL[i], cur_Q[i]], [b2])
                            for i in range(4):
                                b2, c2 = reg(i, 2)
                                TT("dve", nQ[i][:], b2[:, c2:c2 + 128], cur_Q[i][:], ALU.add, [b2, cur_Q[i]], [nQ[i]])
                            cur_L, cur_P, cur_Q = nL, nP, nQ
                        Qf = cur_Q
                        for i, h in enumerate(hs):
                            vh = v_[:, h * 64:(h + 1) * 64]
                            MM(bank[4][:, i * 128:i * 128 + 64], Mh[i][:, 256:384], vh, True, True, [Mh[i], v_], [bank[4]])
                            MM(bank[4][:, i * 128 + 64:(i + 1) * 128], Mh[i][:, 384:512], vh, True, True, [Mh[i], v_], [bank[4]])
                        for i in range(4):
                            CPY("act", LMVr[i][:], bank[4][:, i * 128:(i + 1) * 128], [bank[4]], [LMVr[i]])
                        for hp_ in halves:
                            sl = slice(hp_ * 64, hp_ * 64 + 64)
                            for i, h in enumerate(hs):
                                m = h // 2; par = h % 2
                                MM(bank[i][:, 0:64], AR32[par][:, m, 0:128], HSb[h].ap, True, True, [AR32[par], HSb[h]], [bank[i]])
                            for i, h in enumerate(hs):
                                W_ = Wp[2 * i + hp_]
                                TT("dve", W_[sl, :], bank[i][sl, 0:64], LMVr[i][sl, 0:64], ALU.add, [bank[i], LMVr[i]], [W_])
                            for i, h in enumerate(hs):
                                MM(bank[i][:, 64:128], Qf[i][:], Wp[2 * i + hp_][:], True, True, [Qf[i], Wp[2 * i + hp_]], [bank[i]])
                            for i, h in enumerate(hs):
                                CPY("act", Up[2 * i + hp_][sl, :], bank[i][sl, 64:128], [bank[i]], [Up[2 * i + hp_]])
                            for i, h in enumerate(hs):
                                m = h // 2; par = h % 2
                                MM(bank[i][:, 128:192], AR32[par][:, m, 128:256], HSb[h].ap, True, True, [AR32[par], HSb[h]], [bank[i]])
                                MM(bank[i][:, 192:256], Mh[i][:, 128:256], Up[2 * i + hp_][:], True, True, [Mh[i], Up[2 * i + hp_]], [bank[i]])
                                MM(bank[i][:, 256:320], Kh[:, m * 128:(m + 1) * 128], Vp[:, hp_, h * 64:(h + 1) * 64], True, False, [Kh, Vp], [bank[i]])
                                MM(bank[i][:, 256:320], Bh[:, m * 128:(m + 1) * 128], Up[2 * i + hp_][:], False, True, [Bh, Up[2 * i + hp_]], [bank[i]])
                            for i, h in enumerate(hs):
                                tY = tYr[2 * i + hp_]
                                TT("dve", tY[sl, :], bank[i][sl, 128:192], LMVr[i][sl, 64:128], ALU.add, [bank[i], LMVr[i]], [tY])
                                TT("dve", Y_[sl, h * 64:(h + 1) * 64], bank[i][sl, 192:256], tY[sl, :], ALU.add, [bank[i], tY], [Y_])
                            for i, h in enumerate(hs):
                                m = h // 2; jl = (h % 2) * 64; jh = jl + 64
                                STT("dve", Hp[jl:jh, h, :], Hp[jl:jh, h, :], gam[jl:jh, 2 * m + hp_:2 * m + hp_ + 1], bank[i][jl:jh, 256:320],
                                    ALU.mult, ALU.add, [HSb[h], gam, bank[i]], [HSb[h]])
                    P.dma(rws[f"y{d}"][r0:r0 + 128, :], Y_[:], reads=[Y_], writes=[D_rw[f"y{d}"]], q="pool")
            if rw_stop < 4:
                return
            new_phase()
            lg_ = A.alloc([128, D], F32, "lg_"); lb_ = A.alloc([128, D], F32, "lb_")
            load_row_bc(lg_, pw["rw_lnx_g"].rearrange("(o d) -> o d", o=1), q="act")
            load_row_bc(lb_, pw["rw_lnx_b"].rearrange("(o d) -> o d", o=1), q="act")
            y0 = A.ring(2, [128, D], F32, "y0"); y1 = A.ring(2, [128, D], F32, "y1"); vv = A.ring(2, [128, D], F32, "vv"); gg = A.ring(2, [128, D], F32, "gg")
            bn = A.ring(2, [128, 16], F32, "bn")
            st_ = A.ring(2, [128, 16, 6], F32, "st_"); mv_ = A.ring(2, [128, 16, 2], F32, "mv_"); rs_ = A.ring(2, [128, 32], F32, "rs_")
            ob = A.ring(2, [128, D], BF16, "ob"); oTt = A.ring(2, [128, 8, 128], BF16, "oTt")
            nt_out = NT if want_ctx else 32
            for t in range(nt_out):
                k = t % 2; r0 = t * 128
                a_, b_, v_, g_, bo = y0[k], y1[k], vv[k], gg[k], bn[k]
                P.dma(a_[:], rws["y0"][r0:r0 + 128, :], reads=[D_rw["y0"]], writes=[a_])
                P.dma(b_[:], rws["y1"][r0:r0 + 128, :], reads=[D_rw["y1"]], writes=[b_], q="act")
                P.dma(v_[:], rws["v"][r0:r0 + 128, :], reads=[D_rw["v"]], writes=[v_])
                P.dma(g_[:], rws["g"][r0:r0 + 128, :], reads=[D_rw["g"]], writes=[g_], q="act")
                P.dma(bo[:], bon_scr[r0:r0 + 128, :], reads=[D_bon], writes=[bo], q="act")
                TT("pool", a_[:], a_[:], b_[:], ALU.add, [a_, b_], [a_])
                s_ = st_[k]; m_ = mv_[k]; q_ = rs_[k]
                for hd in range(16):
                    P.dve(lambda e, s_=s_, a_=a_, hd=hd: e.bn_stats(out=s_[:, hd, :], in_=a_[:, hd * 64:(hd + 1) * 64]), [a_], [s_])
                    P.dve(lambda e, s_=s_, m_=m_, hd=hd: e.bn_aggr(out=m_[:, hd, :], in_=s_[:, hd:hd + 1, :]), [s_], [m_])
                ACTF(q_[:, 0:16], m_[:, :, 1], AF.Sqrt, [m_], [q_], bias=64e-5)
                RECIP(q_[:, 16:32], q_[:, 0:16], [q_], [q_])
                for hd in range(16):
                    TS("dve" if hd % 2 else "pool", b_[:, hd * 64:(hd + 1) * 64], a_[:, hd * 64:(hd + 1) * 64], m_[:, hd, 0:1], q_[:, 16 + hd:17 + hd], ALU.subtract, ALU.mult, [a_, m_, q_], [b_])
                TT("pool", b_[:], b_[:], lg_[:], ALU.mult, [b_, lg_], [b_])
                TT("pool", b_[:], b_[:], lb_[:], ALU.add, [b_, lb_], [b_])
                for hd in range(16):
                    STT("dve", b_[:, hd * 64:(hd + 1) * 64], v_[:, hd * 64:(hd + 1) * 64], bo[:, hd:hd + 1], b_[:, hd * 64:(hd + 1) * 64], ALU.mult, ALU.add, [v_, bo, b_], [b_])
                o16 = ob[k]
                TT("pool", o16[:], b_[:], g_[:], ALU.mult, [b_, g_], [o16])
                ptb = bank[k]
                pv = ptb.ap.bitcast(BF16).rearrange("p (a b) -> p a b", a=8)
                for c in range(8):
                    TRN(pv[:, c, :], o16[:, c * 128:(c + 1) * 128], [o16], [ptb])
                ot_ = oTt[k]
                CPY("act", ot_[:], pv, [ptb], [ot_])
                P.dma(oT_scr.rearrange("m p t -> p m t")[:, :, r0:r0 + 128], ot_[:], reads=[ot_], writes=[D_oT], q="pool")
            out_proj_phase(L, want_ctx)

        setup_modulation()
        if rw_only:
            rwkv_phase(0, 0, True)
            n_layers = 0
        for L in range(n_layers):
            kind, slot = L % 3, L // 3
            want_ctx = L < DEPTH - 1
            if kind == 0:
                gqa_phase(L, slot, want_ctx)
            elif kind == 1:
                na_phase(L, slot, want_ctx)
            else:
                rwkv_phase(L, slot, want_ctx)
            mlp_phase(L, want_ctx, final=(L == DEPTH - 1))
        if debug and n_layers > 0:
            new_phase()
            cp = A.ring(3, [128, D], F32, "cp")
            Lr = n_layers - 1
            for t in range(NT):
                b = cp[t % 3]
                P.dma(b[:], xbuf[Lr % 2][t * 128:(t + 1) * 128, :], reads=[D_xb[Lr % 2]], writes=[b])
                P.dma(dbg_out[t * 128:(t + 1) * 128, :], b[:], reads=[b], writes=[D_dbg], q="pool")
        print("instr counts", P.cnt, "dma", {k: v // 16 for k, v in P.dma_val.items()} if False else "")
        P.finish(st)
    return nc


def make_in_maps(inputs, cores):
    cos, sin = rope_tables()
    nab = na_bias_table(np.asarray(inputs["na_rpb"][0], np.float32))
    rwc = rwkv_consts()
    maps = []
    for b in cores:
        m = {
            "x": np.ascontiguousarray(inputs["x"][b]),
            "ctx": np.ascontiguousarray(inputs["ctx"][b]),
            "cc": np.ascontiguousarray(np.stack([inputs["c"][b], inputs["c_ctx"]], axis=0)),
            "rope_cos": cos, "rope_sin": sin, "na_bias": nab, "rwc": rwc,
        }
        for n in W_NAMES:
            m[n] = np.ascontiguousarray(inputs[n])
        maps.append(m)
    return maps


def kernel(**inputs):
    inputs = {k: np.asarray(v) for k, v in inputs.items()}
    nc = build()
    cores = list(range(8))
    res = run_bass_kernel_spmd(nc, make_in_maps(inputs, cores), core_ids=cores)
    return np.stack([r["y"] for r in res.results], axis=0).astype(np.float32)
```

```python
import os
import numpy as np
from contextlib import ExitStack
import concourse.bass as bass
import concourse.mybir as mybir
from concourse.bass_utils import run_bass_kernel_spmd

F32 = mybir.dt.float32
BF16 = mybir.dt.bfloat16
AF = mybir.ActivationFunctionType
ALU = mybir.AluOpType
AX = mybir.AxisListType

D = 1024
NLAT = 4096
NCTX = 256
T = NLAT + NCTX
NT = T // 128
DEPTH = 4
DFF = 4096
ALPHA = (2.0 * DEPTH) ** 0.25
LN_EPS = 1e-6
GRID_W = 64

EPOCH = 8000
NSLOT = 12


class Buf:
    __slots__ = ("ap", "w", "r", "name", "excl")

    def __init__(self, ap, name=""):
        self.ap = ap
        self.w = None
        self.r = {}
        self.name = name
        self.excl = False

    def __getitem__(self, k):
        return self.ap[k]


class Prog:
    ENGS = ("pe", "act", "dve", "pool", "sp")

    def __init__(self, nc):
        self.nc = nc
        self.q = {e: [] for e in self.ENGS}
        self.cnt = {e: 0 for e in self.ENGS}
        self.know = {e: {} for e in self.ENGS}
        self.dma_rr = {e: 0 for e in self.ENGS}
        self.dma_val = {}
        self.pending = {}

    def barrier(self):
        toks = [("E", e, self.cnt[e]) for e in self.ENGS if self.cnt[e]]
        toks += [("D", e, s, v) for (e, s), v in self.dma_val.items()]
        self.pending = {e: list(toks) for e in self.ENGS}

    def _op(self, eng, fn, reads, writes, dma, fence=False):
        deps = []
        if eng in self.pending:
            deps.extend(self.pending.pop(eng))
        for b in reads:
            if b.w is not None:
                deps.append(b.w)
            if b.excl:
                deps.extend(t for k_, t in b.r.items() if k_ != ("E", eng))
        for b in writes:
            if b.w is not None:
                deps.append(b.w)
            deps.extend(b.r.values())
        slot = None
        if dma:
            slot = self.dma_rr[eng]
            self.dma_rr[eng] = (slot + 1) % NSLOT
            prev = self.dma_val.get((eng, slot), 0)
            if prev:
                deps.append(("D", eng, slot, prev))
        waits = []
        kn = self.know[eng]
        for t in deps:
            if t[0] == "E":
                if t[1] == eng and eng == "pe":
                    continue
                key = ("E", t[1])
                val = t[2]
            else:
                key = ("D", t[1], t[2])
                val = t[3]
            if kn.get(key, 0) >= val:
                continue
            kn[key] = val
            waits.append((key, val))
        if fence and self.cnt[eng] and kn.get(("E", eng), 0) < self.cnt[eng]:
            kn[("E", eng)] = self.cnt[eng]
            waits.append((("E", eng), self.cnt[eng]))
        if dma:
            val = self.dma_val.get((eng, slot), 0) + 16
            self.dma_val[(eng, slot)] = val
            tok = ("D", eng, slot, val)
            rkey = ("D", eng, slot)
        else:
            self.cnt[eng] += 1
            tok = ("E", eng, self.cnt[eng])
            rkey = ("E", eng)
        self.q[eng].append((waits, fn, tok))
        for b in reads:
            b.r[rkey] = tok
        for b in writes:
            b.w = tok
            b.r = {}
        return tok

    def pe(self, fn, reads=(), writes=(), fence=False):
        return self._op("pe", fn, reads, writes, False, fence)

    def act(self, fn, reads=(), writes=()):
        return self._op("act", fn, reads, writes, False)

    def dve(self, fn, reads=(), writes=()):
        return self._op("dve", fn, reads, writes, False)

    def pool(self, fn, reads=(), writes=()):
        return self._op("pool", fn, reads, writes, False)

    def dma(self, out_ap, in_ap, reads=(), writes=(), q="sp"):
        return self._op(q, lambda e: e.dma_start(out=out_ap, in_=in_ap), reads, writes, True)

    def finish(self, stack):
        nc = self.nc
        sems = {}

        def sem_for(key, val):
            if key[0] == "E":
                k = (val - 1) // EPOCH
                skey = ("E", key[1], k)
                v = val - k * EPOCH
            else:
                skey = key
                v = val
            if skey not in sems:
                sems[skey] = stack.enter_context(nc.semaphore("s_" + "_".join(str(x) for x in skey)))
            return sems[skey], v

        final = []
        for e in self.ENGS:
            if self.cnt[e]:
                final.append((("E", e), self.cnt[e]))
        for (e, slot), v in self.dma_val.items():
            final.append((("D", e, slot), v))
        finalw = [(k, v) for (k, v) in final if self.know["sp"].get(k, 0) < v]
        block = stack.enter_context(nc.Block())
        prog = self

        def replay(eng_name, engine):
            for waits, fn, tok in prog.q[eng_name]:
                for key, val in waits:
                    s, v = sem_for(key, val)
                    engine.wait_ge(s, v)
                ins = fn(engine)
                if tok[0] == "E":
                    s, v = sem_for(("E", tok[1]), tok[2])
                    ins.then_inc(s, 1)
                else:
                    s, v = sem_for(("D", tok[1], tok[2]), tok[3])
                    ins.then_inc(s, 16)
            if eng_name == "sp":
                for key, val in finalw:
                    s, v = sem_for(key, val)
                    engine.wait_ge(s, v)

        @block.tensor
        def _(e):
            replay("pe", e)

        @block.scalar
        def _(e):
            replay("act", e)

        @block.vector
        def _(e):
            replay("dve", e)

        @block.gpsimd
        def _(e):
            replay("pool", e)

        @block.sync
        def _(e):
            replay("sp", e)


class Arena:
    def __init__(self, ap, nwords):
        self.ap = ap
        self.n = nwords
        self.off = 0

    def reset(self):
        self.off = 0

    def alloc(self, shape, dt, name=""):
        free = 1
        for s in shape[1:]:
            free *= s
        words = free if dt == F32 else (free + 1) // 2
        words = (words + 7) // 8 * 8
        assert self.off + words <= self.n, f"arena overflow {name} {self.off}+{words}>{self.n}"
        v = self.ap[:, self.off:self.off + words]
        self.off += words
        if dt != F32:
            v = v.bitcast(dt)
        v = v[:, 0:free]
        if len(shape) == 3:
            v = v.rearrange("p (a b) -> p a b", a=shape[1])
        elif len(shape) == 4:
            v = v.rearrange("p (a b c) -> p a b c", a=shape[1], b=shape[2])
        return Buf(v, name)

    def ring(self, n, shape, dt, name=""):
        return [self.alloc(shape, dt, f"{name}{i}") for i in range(n)]


def rope_tables():
    t = np.arange(NLAT)
    row = (t // GRID_W).astype(np.float32)
    col = (t % GRID_W).astype(np.float32)
    half = 64
    freqs = (np.float32(10000.0) ** (-np.arange(0, half, 2, dtype=np.float32) / np.float32(half))).astype(np.float32)
    ang_r = row[:, None] * freqs[None, :]
    ang_c = col[:, None] * freqs[None, :]
    ang = np.concatenate([ang_r, ang_r, ang_c, ang_c], axis=-1).astype(np.float32)
    cos = np.cos(ang).astype(np.float32).T.copy()
    sin = np.sin(ang).astype(np.float32).T.copy()
    sign = np.ones((128, 1), np.float32)
    sign[0:32] = -1.0
    sign[64:96] = -1.0
    return cos, (sin * sign).astype(np.float32)


def na_bias_table(rpb):
    NEG = np.float32(-30000.0)
    out = np.full((16, 5, 128, 640), NEG, np.float32)
    ps = [0, 1, 2, 30, 31]
    c = np.arange(64)
    c0 = np.clip(c - 8, 0, 48)
    cp = np.arange(64)
    valid_c = (cp[None, :] >= c0[:, None]) & (cp[None, :] < c0[:, None] + 16)
    dc = np.clip(cp[None, :] - c[:, None] + 15, 0, 30)
    for ti, p in enumerate(ps):
        ws = min(max(2 * p - 4, 0), 54)
        for rl in range(2):
            r = 2 * p + rl
            r0 = min(max(r - 4, 0), 56)
            for j in range(10):
                kr = ws + j
                if not (r0 <= kr < r0 + 8):
                    continue
                dr = kr - r + 7
                vals = rpb[:, dr, :][:, dc]
                blk = np.where(valid_c[None], vals, NEG)
                out[:, ti, rl * 64:(rl + 1) * 64, j * 64:(j + 1) * 64] = blk
    return out


def rwkv_consts():
    u = np.arange(128)
    same = (u[:, None] // 64) == (u[None, :] // 64)
    out = np.zeros((128, 1282), np.float32)
    for d in range(2):
        le = (u[:, None] <= u[None, :]) if d == 0 else (u[:, None] >= u[None, :])
        lt = (u[:, None] < u[None, :]) if d == 0 else (u[:, None] > u[None, :])
        o = d * 640
        out[:, o:o + 128] = same & le
        out[:, o + 128:o + 256] = same & lt.T
        out[:, o + 256:o + 384] = same & lt
        out[:, o + 384:o + 512] = same & le
        out[:, o + 512:o + 640] = same & lt.T
    out[:, 1280] = (u < 64)
    out[:, 1281] = (u >= 64)
    return out


W_NAMES = ["mod_w", "mod_b", "post_ln_g", "post_ln_b", "mlp_w1", "mlp_w2", "att_wqkv", "att_wo", "att_q_norm",
           "att_k_norm", "na_wqkv", "na_wo", "na_rpb", "rw_mu", "rw_wr", "rw_wk", "rw_wv", "rw_wo", "rw_w0", "rw_w1",
           "rw_w2", "rw_a0", "rw_a1", "rw_a2", "rw_g1", "rw_g2", "rw_k_k", "rw_k_a", "rw_r_k", "rw_lnx_g", "rw_lnx_b"]
W_SHAPES = {
    "mod_w": [4, 1024, 6144], "mod_b": [4, 6144], "post_ln_g": [4, 2, 1024], "post_ln_b": [4, 2, 1024],
    "mlp_w1": [4, 1024, 4096], "mlp_w2": [4, 4096, 1024], "att_wqkv": [2, 1024, 1536], "att_wo": [2, 1024, 1024],
    "att_q_norm": [2, 128], "att_k_norm": [2, 128], "na_wqkv": [1, 1024, 3072], "na_wo": [1, 1024, 1024],
    "na_rpb": [1, 16, 15, 31], "rw_mu": [1, 6, 1024], "rw_wr": [1, 1024, 1024], "rw_wk": [1, 1024, 1024],
    "rw_wv": [1, 1024, 1024], "rw_wo": [1, 1024, 1024], "rw_w0": [1, 2, 1024], "rw_w1": [1, 2, 1024, 64],
    "rw_w2": [1, 2, 64, 1024], "rw_a0": [1, 2, 1024], "rw_a1": [1, 2, 1024, 64], "rw_a2": [1, 2, 64, 1024],
    "rw_g1": [1, 1024, 128], "rw_g2": [1, 128, 1024], "rw_k_k": [1, 1024], "rw_k_a": [1, 1024],
    "rw_r_k": [1, 16, 64], "rw_lnx_g": [1, 1024], "rw_lnx_b": [1, 1024],
}


def build(n_layers=DEPTH, debug=False, rw_only=False, rw_stop=99):
    nc = bass.Bass("TRN2", target_bir_lowering=False)
    din = {}
    x_in = nc.dram_tensor("x", [NLAT, D], F32, kind="ExternalInput").ap()
    ctx_in = nc.dram_tensor("ctx", [NCTX, D], F32, kind="ExternalInput").ap()
    cc_in = nc.dram_tensor("cc", [2, D], F32, kind="ExternalInput").ap()
    for n in W_NAMES:
        din[n] = nc.dram_tensor(n, W_SHAPES[n], F32, kind="ExternalInput").ap()
    rope_cos = nc.dram_tensor("rope_cos", [128, NLAT], F32, kind="ExternalInput").ap()
    rope_sin = nc.dram_tensor("rope_sin", [128, NLAT], F32, kind="ExternalInput").ap()
    na_bias = nc.dram_tensor("na_bias", [16, 5, 128, 640], F32, kind="ExternalInput").ap()
    rwc_in = nc.dram_tensor("rwc", [128, 1282], F32, kind="ExternalInput").ap()
    y_out = nc.dram_tensor("y", [NLAT, D], F32, kind="ExternalOutput").ap()
    dbg_out = nc.dram_tensor("dbg", [T, D], F32, kind="ExternalOutput").ap() if debug else None

    xbuf = [nc.dram_tensor(f"xs{i}", [T, D], F32, kind="Internal").ap() for i in range(2)]
    x1s = nc.dram_tensor("x1s", [T, D], F32, kind="Internal").ap()
    mscr = nc.dram_tensor("mscr", [DEPTH, 2, 6 * D], F32, kind="Internal").ap()
    w1b = nc.dram_tensor("w1b", [D, DFF], BF16, kind="Internal").ap()
    w2b = nc.dram_tensor("w2b", [DFF, D], BF16, kind="Internal").ap()
    wqkvb = nc.dram_tensor("wqkvb", [D, 3072], BF16, kind="Internal").ap()
    wob = nc.dram_tensor("wob", [D, D], BF16, kind="Internal").ap()
    qT_scr = nc.dram_tensor("qT_scr", [8, 128, T], BF16, kind="Internal").ap()
    kT_scr = nc.dram_tensor("kT_scr", [8, 128, T], BF16, kind="Internal").ap()
    oT_scr = nc.dram_tensor("oT_scr", [8, 128, T], BF16, kind="Internal").ap()
    v_scr = nc.dram_tensor("v_scr", [T, D], BF16, kind="Internal").ap()
    o_scr = nc.dram_tensor("o_scr", [T, D], BF16, kind="Internal").ap()
    D_o = Buf(o_scr)
    RW_NAMES = ["hs", "r", "v", "kk", "g", "kd0", "kd1", "b0", "b1", "lw0", "lw1"]
    rws = {n: nc.dram_tensor("rws_" + n, [T, D], F32, kind="Internal").ap() for n in RW_NAMES}
    D_rw = {n: Buf(rws[n]) for n in RW_NAMES}
    rws["y0"] = rws["hs"]; D_rw["y0"] = D_rw["hs"]
    rws["y1"] = rws["kd0"]; D_rw["y1"] = D_rw["kd0"]
    bon_scr = nc.dram_tensor("bon_scr", [T, 16], F32, kind="Internal").ap()
    D_bon = Buf(bon_scr)

    D_x = Buf(x_in); D_ctx = Buf(ctx_in)
    D_xb = [Buf(xbuf[0]), Buf(xbuf[1])]
    D_x1 = Buf(x1s); D_m = Buf(mscr)
    D_w1b = Buf(w1b); D_w2b = Buf(w2b); D_wqkvb = Buf(wqkvb); D_wob = Buf(wob); D_qT = Buf(qT_scr)
    D_kT = Buf(kT_scr); D_oT = Buf(oT_scr); D_v = Buf(v_scr)
    D_y = Buf(y_out); D_dbg = Buf(dbg_out) if debug else None
    D_const = Buf(None)

    with ExitStack() as st:
        P = Prog(nc)

        def sb(name, shape, dt):
            return Buf(st.enter_context(nc.sbuf_tensor(name, shape, dt)), name)

        ident16 = sb("ident16", [128, 128], BF16)
        ident32 = sb("ident32", [128, 128], F32)
        ones32 = sb("ones32", [128, 128], F32)
        ones16 = sb("ones16", [128, 128], BF16)
        condT = sb("condT", [128, 8, 2], F32)
        ARENA_WORDS = 51200
        arena_t = st.enter_context(nc.sbuf_tensor("arena", [128, ARENA_WORDS], F32))
        A = Arena(arena_t, ARENA_WORDS)
        pst = [st.enter_context(nc.psum_tensor(f"ps{k}", [128, 1024], F32)) for k in range(4)]
        bank = []
        for k in range(4):
            bank.append(Buf(pst[k][:, 0:512], f"bank{2 * k}"))
            bank.append(Buf(pst[k][:, 512:1024], f"bank{2 * k + 1}"))
        for b_ in bank:
            b_.excl = True

        def new_phase():
            P.barrier()
            A.reset()


        def eng_fn(eng):
            return {"act": P.act, "dve": P.dve, "pool": P.pool, "pe": P.pe}[eng]

        mm_state = [False]

        def MM(out, lhsT, rhs, start, stop, R, W, part=False):
            fence = part or mm_state[0]
            mm_state[0] = part
            P.pe(lambda e: e.matmul(out, lhsT=lhsT, rhs=rhs, start=start, stop=stop), R, W, fence=fence)

        def TRN(out, in_, R, W, ident=None):
            idn = ident16[:] if ident is None else ident
            P.pe(lambda e: e.transpose(out, in_, idn), list(R) + [ident16], W)

        def ACTF(out, in_, func, R, W, bias=0.0, scale=1.0, accum=None):
            if accum is None:
                P.act(lambda e: e.activation(out=out, in_=in_, func=func, bias=bias, scale=scale), R, W)
            else:
                P.act(lambda e: e.activation(out=out, in_=in_, func=func, bias=bias, scale=scale, accum_out=accum), R, W)

        def TT(eng, out, in0, in1, op, R, W):
            eng_fn(eng)(lambda e: e.tensor_tensor(out=out, in0=in0, in1=in1, op=op), R, W)

        def STT(eng, out, in0, scalar, in1, op0, op1, R, W):
            eng_fn(eng)(lambda e: e.scalar_tensor_tensor(out=out, in0=in0, scalar=scalar, in1=in1, op0=op0, op1=op1), R, W)

        def TS(eng, out, in0, s1, s2, op0, op1, R, W):
            if s2 is None:
                eng_fn(eng)(lambda e: e.tensor_scalar(out=out, in0=in0, scalar1=s1, scalar2=None, op0=op0), R, W)
            else:
                eng_fn(eng)(lambda e: e.tensor_scalar(out=out, in0=in0, scalar1=s1, scalar2=s2, op0=op0, op1=op1), R, W)

        def CPY(eng, out, in_, R, W):
            if eng == "act":
                P.act(lambda e: e.copy(out=out, in_=in_), R, W)
            else:
                eng_fn(eng)(lambda e: e.tensor_copy(out=out, in_=in_), R, W)

        def RECIP(out, in_, R, W):
            P.dve(lambda e: e.reciprocal(out=out, in_=in_), R, W)

        P.pool(lambda e: e.memset(ident32[:], 1.0), [], [ident32])
        P.pool(lambda e: e.affine_select(out=ident32[:], in_=ident32[:], pattern=[[-1, 128]], compare_op=ALU.is_equal,
                                         fill=0.0, base=0, channel_multiplier=1), [ident32], [ident32])
        P.pool(lambda e: e.tensor_copy(out=ident16[:], in_=ident32[:]), [ident32], [ident16])
        P.pool(lambda e: e.memset(ones32[:], 1.0), [], [ones32])
        P.pool(lambda e: e.memset(ones16[:], 1.0), [], [ones16])

        def src_rows(layer, r0, n):
            if layer == 0:
                if r0 < NLAT:
                    return x_in[r0:r0 + n, :], D_x
                return ctx_in[r0 - NLAT:r0 - NLAT + n, :], D_ctx
            bsel = (layer - 1) % 2
            return xbuf[bsel][r0:r0 + n, :], D_xb[bsel]

        def dst_rows(layer, r0, n):
            bsel = layer % 2
            return xbuf[bsel][r0:r0 + n, :], D_xb[bsel]

        def load_bc(buf, layer, r, j, q="sp"):
            P.dma(buf[:], mscr[layer, r:r + 1, j * D:(j + 1) * D].partition_broadcast(128), reads=[D_m], writes=[buf], q=q)

        def load_row_bc(buf, row_ap, q="sp"):
            P.dma(buf[:], row_ap.partition_broadcast(128), reads=[], writes=[buf], q=q)

        def cast_weight(src, dst, dstbuf, rows, cols, ring32, ring16, ctr):
            cb = 2048
            for r in range(rows // 128):
                for c0 in range(0, cols, cb):
                    cw = min(cb, cols - c0)
                    i = ctr[0]
                    ctr[0] += 1
                    b32 = ring32[i % len(ring32)]
                    b16 = ring16[i % len(ring16)]
                    P.dma(b32[:, 0:cw], src[r * 128:(r + 1) * 128, c0:c0 + cw], reads=[], writes=[b32], q="sp")
                    CPY("pool" if i % 2 == 0 else "act", b16[:, 0:cw], b32[:, 0:cw], [b32], [b16])
                    P.dma(dst[r * 128:(r + 1) * 128, c0:c0 + cw], b16[:, 0:cw], reads=[b16], writes=[dstbuf], q="pool")

        def layernorm(z, gbc, bbc, out, st6, mv, rs, tmp):
            P.dve(lambda e: e.bn_stats(out=st6[:, 0:6], in_=z[:, 0:512]), [z], [st6])
            P.dve(lambda e: e.bn_stats(out=st6[:, 6:12], in_=z[:, 512:1024]), [z], [st6])
            P.dve(lambda e: e.bn_aggr(out=mv[:], in_=st6[:].rearrange("p (a b) -> p a b", a=2)), [st6], [mv])
            ACTF(rs[:, 0:1], mv[:, 1:2], AF.Sqrt, [mv], [rs], bias=LN_EPS, scale=1.0)
            RECIP(rs[:, 1:2], rs[:, 0:1], [rs], [rs])
            TS("dve", tmp[:], z[:], mv[:, 0:1], rs[:, 1:2], ALU.subtract, ALU.mult, [z, mv, rs], [tmp])
            TT("pool", tmp[:], tmp[:], gbc[:], ALU.mult, [tmp, gbc], [tmp])
            TT("pool", out[:], tmp[:], bbc[:], ALU.add, [tmp, bbc], [out])

        def modulate_transpose(xt, scp, sh, h1, h2, psT, hT, t):
            TT("dve", h1[:], xt[:], scp[:], ALU.mult, [xt, scp], [h1])
            TT("pool", h2[:], h1[:], sh[:], ALU.add, [h1, sh], [h2])
            pv = psT.ap.bitcast(BF16).rearrange("p (a b) -> p a b", a=8)
            for c in range(8):
                TRN(pv[:, c, :], h2[:, c * 128:(c + 1) * 128], [h2], [psT])
            CPY("act", hT[:, :, t * 128:(t + 1) * 128], pv, [psT], [hT])

        def setup_modulation():
            new_phase()
            cc = A.alloc([128, D], F32, "cc")
            sil = A.alloc([128, D], F32, "sil")
            msb = A.alloc([128, 6 * D], F32, "msb")
            mb2 = A.alloc([128, 6 * D], F32, "mb2")
            wn = A.ring(2, [128, 8, 512], F32, "wn")
            P.dma(cc[0:2, :], cc_in[:, :], reads=[], writes=[cc])
            ACTF(sil[0:2, :], cc[0:2, :], AF.Silu, [cc], [sil])
            for c in range(8):
                MM(bank[0][:, 0:2], sil[0:2, c * 128:(c + 1) * 128], ident32[0:2, 0:2], True, True, [sil, ident32], [bank[0]])
                CPY("dve", condT[:, c, :], bank[0][:, 0:2], [bank[0]], [condT])
            for L in range(DEPTH):
                P.dma(mb2[0:2, :], din["mod_b"][L:L + 1, :].partition_broadcast(2), reads=[], writes=[mb2], q="act")
                for n in range(12):
                    w = wn[n % 2]
                    P.dma(w[:], din["mod_w"][L].rearrange("(c p) n -> p c n", p=128)[:, :, n * 512:(n + 1) * 512],
                          reads=[], writes=[w], q="sp" if n % 2 == 0 else "act")
                    pb = bank[1 + n % 2]
                    for c in range(8):
                        MM(pb[0:2, :], condT[:, c, :], w[:, c, :], c == 0, c == 7, [condT, w], [pb])
                    TT("dve", msb[0:2, n * 512:(n + 1) * 512], pb[0:2, :], mb2[0:2, n * 512:(n + 1) * 512], ALU.add, [pb, mb2], [msb])
                for j in (1, 4):
                    TS("dve", msb[0:2, j * D:(j + 1) * D], msb[0:2, j * D:(j + 1) * D], 1.0, None, ALU.add, None, [msb], [msb])
                P.dma(mscr[L, :, :], msb[0:2, :], reads=[msb], writes=[D_m], q="pool")

        def mlp_phase(L, want_ctx, final):
            new_phase()
            r32 = A.ring(3, [128, 2048], F32, "c32")
            r16 = A.ring(3, [128, 2048], BF16, "c16")
            ctr = [0]
            cast_weight(din["mlp_w1"][L], w1b, D_w1b, D, DFF, r32, r16, ctr)
            cast_weight(din["mlp_w2"][L], w2b, D_w2b, DFF, D, r32, r16, ctr)
            new_phase()
            bc = {}
            for r in range(2 if want_ctx else 1):
                for j in (3, 4, 5):
                    bc[(r, j)] = A.alloc([128, D], F32, f"bc{r}{j}")
                    load_bc(bc[(r, j)], L, r, j, q="act")
            g2 = A.alloc([128, D], F32, "g2"); b2 = A.alloc([128, D], F32, "b2")
            load_row_bc(g2, din["post_ln_g"][L, 1:2, :], q="act")
            load_row_bc(b2, din["post_ln_b"][L, 1:2, :], q="act")
            hT = A.alloc([128, 8, 512], BF16, "hT")
            uT = A.alloc([128, 32, 512], BF16, "uT")
            w1g = A.ring(2, [128, 8, 512], BF16, "w1g")
            w2g = A.ring(2, [128, 4, 512], BF16, "w2g")
            x1t = A.ring(4, [128, D], F32, "x1t")
            ysb = A.ring(4, [128, D], F32, "ysb")
            hm = A.ring(2, [128, D], F32, "hm")
            hb = A.ring(2, [128, D], BF16, "hb")
            rl = A.ring(2, [128, 512], F32, "rl")
            zt = A.ring(2, [128, D], F32, "zt")
            tmp = A.ring(2, [128, D], F32, "tmp")
            ot = A.ring(2, [128, D], F32, "ot")
            st6 = A.ring(2, [128, 12], F32, "st6"); mv = A.ring(2, [128, 2], F32, "mv"); rs = A.ring(2, [128, 2], F32, "rs")
            psT = bank[0]
            psU = [bank[1], bank[2]]
            psY = [bank[4], bank[5], bank[6], bank[7]]
            sts = [(s * 512, 512, 0) for s in range(8)]
            if want_ctx:
                sts.append((NLAT, 256, 1))
            k = 0
            wctr = 0
            for (r0, ntok, r) in sts:
                nt = ntok // 128
                scp, sh, gt = bc[(r, 4)], bc[(r, 3)], bc[(r, 5)]
                for t in range(nt):
                    xt = x1t[t]
                    P.dma(xt[:], x1s[r0 + t * 128:r0 + (t + 1) * 128, :], reads=[D_x1], writes=[xt])
                    modulate_transpose(xt, scp, sh, hm[k % 2], hb[k % 2], psT, hT, t)
                    k += 1

                def emit_U(f, wg, ntok=ntok):
                    pu = psU[f % 2]
                    for c in range(8):
                        MM(pu[:, 0:ntok], wg[:, c, (f % 4) * 128:(f % 4 + 1) * 128], hT[:, c, 0:ntok], c == 0, c == 7, [wg, hT], [pu])
                    rr = rl[f % 2]
                    ACTF(rr[:, 0:ntok], pu[:, 0:ntok], AF.Relu, [pu], [rr])
                    TT("pool", uT[:, f, 0:ntok], rr[:, 0:ntok], rr[:, 0:ntok], ALU.mult, [rr], [uT])

                def emit_Y(f, w2, first, last, nt=nt):
                    for t in range(nt):
                        MM(psY[t][:, :], uT[:, f, t * 128:(t + 1) * 128], w2[:, f % 4, :], first, last, [uT, w2], [psY[t]])

                wgs = {}
                for fg in range(8):
                    wa = w1g[wctr % 2]; wb_ = w2g[wctr % 2]; wctr += 1
                    P.dma(wa[:], w1b.rearrange("(c p) f -> p c f", p=128)[:, :, fg * 512:(fg + 1) * 512], reads=[D_w1b], writes=[wa], q="sp")
                    P.dma(wb_[:], w2b.rearrange("(f p) d -> p f d", p=128)[:, fg * 4:(fg + 1) * 4, 0:512], reads=[D_w2b], writes=[wb_], q="act")
                    wgs[fg] = wb_
                    for fi in range(4):
                        f = fg * 4 + fi
                        emit_U(f, wa)
                        if f > 0:
                            emit_Y(f - 1, wgs[(f - 1) // 4], f - 1 == 0, False)
                emit_Y(31, wgs[7], False, True)
                for t in range(nt):
                    TT("dve", ysb[t][:, 0:512], psY[t][:, :], gt[:, 0:512], ALU.mult, [psY[t], gt], [ysb[t]])
                for fg in range(8):
                    wb_ = w2g[wctr % 2]; wctr += 1
                    P.dma(wb_[:], w2b.rearrange("(f p) d -> p f d", p=128)[:, fg * 4:(fg + 1) * 4, 512:1024], reads=[D_w2b], writes=[wb_], q="act")
                    for fi in range(4):
                        f = fg * 4 + fi
                        emit_Y(f, wb_, f == 0, f == 31)
                for t in range(nt):
                    TT("dve", ysb[t][:, 512:1024], psY[t][:, :], gt[:, 512:1024], ALU.mult, [psY[t], gt], [ysb[t]])
                for t in range(nt):
                    z = zt[t % 2]; o = ot[t % 2]
                    TS("pool", z[:], x1t[t][:], ALPHA, None, ALU.mult, None, [x1t[t]], [z])
                    TT("pool", z[:], z[:], ysb[t][:], ALU.add, [z, ysb[t]], [z])
                    layernorm(z, g2, b2, o, st6[t % 2], mv[t % 2], rs[t % 2], tmp[t % 2])
                    rr0 = r0 + t * 128
                    if final:
                        if r == 0:
                            P.dma(y_out[rr0:rr0 + 128, :], o[:], reads=[o], writes=[D_y], q="pool")
                    else:
                        dap, dbuf = dst_rows(L, rr0, 128)
                        P.dma(dap, o[:], reads=[o], writes=[dbuf], q="pool")

        def mixer_epilogue(L, r0, t, psYap, psYb, gt, g1, b1, bufs, k):
            xr, yg, zt, tmp, ot, st6, mv, rs = bufs
            xt = xr[k % 2]; y_ = yg[k % 2]; z = zt[k % 2]; o = ot[k % 2]
            sap, sbuf_ = src_rows(L, r0 + t * 128, 128)
            P.dma(xt[:], sap, reads=[sbuf_], writes=[xt])
            TT("dve", y_[:], psYap[:, :], gt[:], ALU.mult, list(psYb) + [gt], [y_])
            TS("pool", z[:], xt[:], ALPHA, None, ALU.mult, None, [xt], [z])
            TT("pool", z[:], z[:], y_[:], ALU.add, [z, y_], [z])
            layernorm(z, g1, b1, o, st6[k % 2], mv[k % 2], rs[k % 2], tmp[k % 2])
            rr0 = r0 + t * 128
            P.dma(x1s[rr0:rr0 + 128, :], o[:], reads=[o], writes=[D_x1], q="pool")

        def epilogue_bufs():
            return (A.ring(2, [128, D], F32, "xr"), A.ring(2, [128, D], F32, "yg"), A.ring(2, [128, D], F32, "zt"),
                    A.ring(2, [128, D], F32, "tmp"), A.ring(2, [128, D], F32, "ot"), A.ring(2, [128, 12], F32, "st6"),
                    A.ring(2, [128, 2], F32, "mv"), A.ring(2, [128, 2], F32, "rs"))

        def gqa_phase(L, slot, want_ctx):
            new_phase()
            r32 = A.ring(3, [128, 2048], F32, "c32")
            r16 = A.ring(3, [128, 2048], BF16, "c16")
            ctr = [0]
            cast_weight(din["att_wqkv"][slot], wqkvb[:, 0:1536], D_wqkvb, D, 1536, r32, r16, ctr)
            cast_weight(din["att_wo"][slot], wob, D_wob, D, D, r32, r16, ctr)
            new_phase()
            bc = {}
            for r in range(2):
                for j in (0, 1, 2):
                    if j == 2 and r == 1 and not want_ctx:
                        continue
                    bc[(r, j)] = A.alloc([128, D], F32, f"bc{r}{j}")
                    load_bc(bc[(r, j)], L, r, j, q="act")
            g1 = A.alloc([128, D], F32, "g1"); b1 = A.alloc([128, D], F32, "b1")
            load_row_bc(g1, din["post_ln_g"][L, 0:1, :], q="act")
            load_row_bc(b1, din["post_ln_b"][L, 0:1, :], q="act")
            gq = A.alloc([128, 2], F32, "gq")
            P.dma(gq[:, 0:1], din["att_q_norm"][slot].rearrange("(p o) -> p o", o=1), reads=[], writes=[gq], q="act")
            P.dma(gq[:, 1:2], din["att_k_norm"][slot].rearrange("(p o) -> p o", o=1), reads=[], writes=[gq], q="act")
            kT = A.alloc([128, 2, T], BF16, "kT")
            vsb = A.alloc([128, NT, 256], BF16, "vsb")
            off_keep = A.off
            wq = A.alloc([128, 8, 1536], BF16, "wq")
            P.dma(wq[:], wqkvb.rearrange("(c p) n -> p c n", p=128)[:, :, 0:1536], reads=[D_wqkvb], writes=[wq])
            hT = A.alloc([128, 8, 512], BF16, "hT")
            xr = A.ring(2, [128, D], F32, "xr")
            hm = A.ring(2, [128, D], F32, "hm")
            hb = A.ring(2, [128, D], BF16, "hb")
            cs = A.ring(2, [128, 512], F32, "cs"); sn = A.ring(2, [128, 512], F32, "sn")
            sq = A.ring(2, [128, 512], F32, "sq"); rt = A.ring(2, [128, 512], F32, "rt"); rcp = A.ring(2, [128, 512], F32, "rcp")
            qn = A.ring(2, [128, 512], F32, "qn"); shf = A.ring(2, [128, 512], F32, "shf")
            t1 = A.ring(2, [128, 512], F32, "t1"); t2 = A.ring(2, [128, 512], F32, "t2")
            qo = A.ring(3, [128, 512], BF16, "qo")
            psT = bank[0]
            psQ = [bank[1], bank[2]]
            psS = [bank[3], bank[4]]
            psV = [bank[5], bank[6]]
            sts = [(s * 512, 512, 0) for s in range(8)] + [(NLAT, 256, 1)]
            k = 0; oi = 0
            for si, (r0, ntok, r) in enumerate(sts):
                nt = ntok // 128
                scp, sh = bc[(r, 1)], bc[(r, 0)]
                for t in range(nt):
                    xt = xr[k % 2]
                    sap, sbuf_ = src_rows(L, r0 + t * 128, 128)
                    P.dma(xt[:], sap, reads=[sbuf_], writes=[xt])
                    modulate_transpose(xt, scp, sh, hm[k % 2], hb[k % 2], psT, hT, t)
                    k += 1
                for t in range(nt):
                    pvv = psV[t % 2]
                    for c in range(8):
                        MM(pvv[:, 0:256], hT[:, c, t * 128:(t + 1) * 128], wq[:, c, 1280:1536], c == 0, c == 7, [hT, wq], [pvv])
                    ti = (r0 // 128) + t
                    CPY("act", vsb[:, ti, :], pvv[:, 0:256], [pvv], [vsb])
                if r == 0:
                    c_ = cs[si % 2]; s_ = sn[si % 2]
                    P.dma(c_[:], rope_cos[:, r0:r0 + 512], reads=[], writes=[c_], q="act")
                    P.dma(s_[:], rope_sin[:, r0:r0 + 512], reads=[], writes=[s_], q="act")
                for oc in range(10):
                    pq = psQ[oc % 2]; pss = psS[oc % 2]
                    sq_ = sq[oc % 2]; rt_ = rt[oc % 2]; rc_ = rcp[oc % 2]; qn_ = qn[oc % 2]
                    for c in range(8):
                        MM(pq[:, 0:ntok], wq[:, c, oc * 128:(oc + 1) * 128], hT[:, c, 0:ntok], c == 0, c == 7, [wq, hT], [pq])
                    ACTF(sq_[:, 0:ntok], pq[:, 0:ntok], AF.Square, [pq], [sq_])
                    MM(pss[:, 0:ntok], ones32[:], sq_[:, 0:ntok], True, True, [ones32, sq_], [pss])
                    ACTF(rt_[:, 0:ntok], pss[:, 0:ntok], AF.Sqrt, [pss], [rt_], bias=LN_EPS, scale=1.0 / 128)
                    RECIP(rc_[:, 0:ntok], rt_[:, 0:ntok], [rt_], [rc_])
                    gcol = gq[:, 0:1] if oc < 8 else gq[:, 1:2]
                    STT("dve", qn_[:, 0:ntok], pq[:, 0:ntok], gcol, rc_[:, 0:ntok], ALU.mult, ALU.mult, [pq, rc_, gq], [qn_])
                    if oc < 8:
                        dest = qo[oi % 3]; oi += 1
                        dap = dest[:, 0:ntok]
                    else:
                        dest = kT
                        dap = kT[:, oc - 8, r0:r0 + ntok]
                    if r == 0:
                        sh_ = shf[oc % 2]; t1_ = t1[oc % 2]; t2_ = t2[oc % 2]
                        CPY("pool", sh_[0:32, :], qn_[32:64, :], [qn_], [sh_])
                        CPY("act", sh_[32:64, :], qn_[0:32, :], [qn_], [sh_])
                        CPY("pool", sh_[64:96, :], qn_[96:128, :], [qn_], [sh_])
                        CPY("act", sh_[96:128, :], qn_[64:96, :], [qn_], [sh_])
                        TT("pool", t1_[:], qn_[:], c_[:], ALU.mult, [qn_, c_], [t1_])
                        TT("pool", t2_[:], sh_[:], s_[:], ALU.mult, [sh_, s_], [t2_])
                        TT("dve", dap, t1_[:], t2_[:], ALU.add, [t1_, t2_], [dest])
                    else:
                        CPY("pool", dap, qn_[:, 0:ntok], [qn_], [dest])
                    if oc < 8:
                        P.dma(qT_scr[oc, :, r0:r0 + ntok], dest[:, 0:ntok], reads=[dest], writes=[D_qT], q="pool")
            P.barrier()
            A.off = off_keep
            wo = A.alloc([128, 8, D], BF16, "wo")
            P.dma(wo[:], wob.rearrange("(c p) n -> p c n", p=128), reads=[D_wob], writes=[wo])
            qT = A.ring(2, [128, 8, 512], BF16, "qT")
            pT = A.ring(4, [128, 512], BF16, "pT")
            aT = A.alloc([128, 8, 512], BF16, "aT")
            rec = A.ring(2, [128, 512], F32, "rec")
            ebufs = epilogue_bufs()
            psS = [bank[0], bank[1]]
            psO = [bank[2], bank[3]]
            psR = [bank[4], bank[5]]
            psY = (bank[6], bank[7])
            psYap = pst[3]
            scale = 128.0 ** -0.5
            sts = [(s * 512, 512, 0) for s in range(8)]
            if want_ctx:
                sts.append((NLAT, 256, 1))
            pi = 0; hi = 0; k = 0
            for si, (r0, ntok, r) in enumerate(sts):
                nt = ntok // 128
                q_ = qT[si % 2]
                P.dma(q_[:, :, 0:ntok], qT_scr.rearrange("h p t -> p h t")[:, :, r0:r0 + ntok], reads=[D_qT], writes=[q_])
                kts = list(range(NT)) if r == 0 else [32, 33]
                for h in range(8):
                    kv = h // 4
                    po = psO[hi % 2]; pr = psR[hi % 2]; rc = rec[hi % 2]; hi += 1
                    def emit_S(ki_, pi_):
                        kt_ = kts[ki_]
                        ps__ = psS[pi_ % 2]
                        MM(ps__[:, 0:ntok], kT[:, kv, kt_ * 128:(kt_ + 1) * 128], q_[:, h, 0:ntok], True, True, [kT, q_], [ps__])

                    emit_S(0, pi)
                    for ki, kt in enumerate(kts):
                        ps_ = psS[pi % 2]; p_ = pT[pi % 4]; pi += 1
                        if ki + 1 < len(kts):
                            emit_S(ki + 1, pi)
                        ACTF(p_[:, 0:ntok], ps_[:, 0:ntok], AF.Exp, [ps_], [p_], scale=scale)
                        first = ki == 0; last = ki == len(kts) - 1
                        MM(po[:, 0:ntok], vsb[:, kt, kv * 128:(kv + 1) * 128], p_[:, 0:ntok], first, last, [vsb, p_], [po])
                        MM(pr[:, 0:ntok], ones16[:], p_[:, 0:ntok], first, last, [ones16, p_], [pr])
                    RECIP(rc[:, 0:ntok], pr[:, 0:ntok], [pr], [rc])
                    TT("dve", aT[:, h, 0:ntok], po[:, 0:ntok], rc[:, 0:ntok], ALU.mult, [po, rc], [aT])
                gt = bc[(r, 2)]
                for t in range(nt):
                    for n in range(2):
                        for h in range(8):
                            MM(psYap[:, n * 512:(n + 1) * 512], aT[:, h, t * 128:(t + 1) * 128], wo[:, h, n * 512:(n + 1) * 512], h == 0, h == 7, [aT, wo], [psY[n]])
                    mixer_epilogue(L, r0, t, psYap, psY, gt, g1, b1, ebufs, k)
                    k += 1

        def na_phase(L, slot, want_ctx):
            new_phase()
            r32 = A.ring(3, [128, 2048], F32, "c32")
            r16 = A.ring(3, [128, 2048], BF16, "c16")
            ctr = [0]
            cast_weight(din["na_wqkv"][slot], wqkvb, D_wqkvb, D, 3072, r32, r16, ctr)
            cast_weight(din["na_wo"][slot], wob, D_wob, D, D, r32, r16, ctr)
            new_phase()
            bc = {}
            for r in range(2):
                for j in (0, 1):
                    bc[(r, j)] = A.alloc([128, D], F32, f"bc{r}{j}")
                    load_bc(bc[(r, j)], L, r, j, q="act")
            wq = A.alloc([128, 8, 3072], BF16, "wq")
            P.dma(wq[:], wqkvb.rearrange("(c p) n -> p c n", p=128), reads=[D_wqkvb], writes=[wq])
            hT = A.alloc([128, 8, 512], BF16, "hT")
            xr = A.ring(2, [128, D], F32, "xr")
            hm = A.ring(2, [128, D], F32, "hm")
            hb = A.ring(2, [128, D], BF16, "hb")
            qo = A.ring(4, [128, 512], BF16, "qo")
            vo = A.ring(3, [128, D], BF16, "vo")
            psT = bank[0]
            psQ = [bank[1], bank[2], bank[3]]
            psVt = [pst[2], pst[3]]
            psVb = [(bank[4], bank[5]), (bank[6], bank[7])]
            sts = [(s * 512, 512, 0) for s in range(8)] + [(NLAT, 256, 1)]
            k = 0; oi = 0
            for si, (r0, ntok, r) in enumerate(sts):
                nt = ntok // 128
                scp, sh = bc[(r, 1)], bc[(r, 0)]
                for t in range(nt):
                    xt = xr[k % 2]
                    sap, sbuf_ = src_rows(L, r0 + t * 128, 128)
                    P.dma(xt[:], sap, reads=[sbuf_], writes=[xt])
                    modulate_transpose(xt, scp, sh, hm[k % 2], hb[k % 2], psT, hT, t)
                    k += 1
                for oc in range(16):
                    pq = psQ[oc % 3]
                    for c in range(8):
                        MM(pq[:, 0:ntok], wq[:, c, oc * 128:(oc + 1) * 128], hT[:, c, 0:ntok], c == 0, c == 7, [wq, hT], [pq])
                    dest = qo[oi % 4]; oi += 1
                    if oc < 8:
                        ACTF(dest[:, 0:ntok], pq[:, 0:ntok], AF.Copy, [pq], [dest], scale=0.125)
                        P.dma(qT_scr[oc, :, r0:r0 + ntok], dest[:, 0:ntok], reads=[dest], writes=[D_qT], q="pool")
                    else:
                        CPY("dve", dest[:, 0:ntok], pq[:, 0:ntok], [pq], [dest])
                        P.dma(kT_scr[oc - 8, :, r0:r0 + ntok], dest[:, 0:ntok], reads=[dest], writes=[D_kT], q="pool")
                for t in range(nt):
                    pvt = psVt[t % 2]; pvb = psVb[t % 2]
                    for n in range(2):
                        for c in range(8):
                            MM(pvt[:, n * 512:(n + 1) * 512], hT[:, c, t * 128:(t + 1) * 128], wq[:, c, 2048 + n * 512:2048 + (n + 1) * 512], c == 0, c == 7, [hT, wq], [pvb[n]])
                    v_ = vo[t % 3]
                    CPY("act" if t % 2 == 0 else "dve", v_[:], pvt[:, :], list(pvb), [v_])
                    P.dma(v_scr[r0 + t * 128:r0 + (t + 1) * 128, :], v_[:], reads=[v_], writes=[D_v], q="pool")
            new_phase()
            qTm = A.ring(2, [128, T], BF16, "qTm")
            kTm = A.ring(2, [128, T], BF16, "kTm")
            vm = A.ring(2, [128, NT, 128], BF16, "vm")
            osb = A.ring(2, [128, NT, 128], BF16, "osb")
            btab = A.ring(2, [128, 10, 640], F32, "btab")
            ssb = A.ring(4, [128, 896], F32, "ssb")
            pn = A.ring(4, [128, 896], BF16, "pn")
            pTs = A.ring(4, [128, 7, 128], BF16, "pTs")
            sm = A.ring(4, [128, 4], F32, "sm")
            psS = [pst[0], pst[1]]
            psSb = [(bank[0], bank[1]), (bank[2], bank[3])]
            psTr = [bank[4], bank[5]]
            psO = [bank[6], bank[7]]
            pidx = [0]

            def attend2(m, q0, segs, bias_aps, btab_buf, vtiles):
                pi_ = pidx[0]; pidx[0] += 1
                q_ = qTm[m % 2]; k_ = kTm[m % 2]; v_ = vm[m % 2]; o_ = osb[m % 2]
                ntot = sum(nk for _, nk in segs)
                nkt = ntot // 128
                nb = 640 if bias_aps is not None else 0
                J = [a + 2 * (pi_ % 2) for a in range(2)]
                for a in range(2):
                    lo = a * 64; hi = lo + 64
                    col = 0
                    for (k0, nk) in segs:
                        bsel = psSb[a][0] if col < 512 else psSb[a][1]
                        MM(psS[a][:, col:col + nk], q_[lo:hi, q0:q0 + 128], k_[lo:hi, k0:k0 + nk], True, True, [q_, k_], [bsel])
                        col += nk
                for a in range(2):
                    s_ = ssb[J[a]]
                    if nb:
                        TT("dve", s_[:, 0:nb], psS[a][:, 0:nb], bias_aps[a], ALU.add, [psSb[a][0], psSb[a][1], btab_buf], [s_])
                    CPY("act", s_[:, nb:ntot], psS[a][:, nb:ntot], list(psSb[a]), [s_])
                for a in range(2):
                    s_ = ssb[J[a]]; sm_ = sm[J[a]]
                    P.dve(lambda e, s_=s_, sm_=sm_: e.reduce_max(out=sm_[:, 0:1], in_=s_[:, 0:ntot], axis=AX.X), [s_], [sm_])
                    TS("dve", sm_[:, 1:2], sm_[:, 0:1], -1.0, None, ALU.mult, None, [sm_], [sm_])
                for a in range(2):
                    s_ = ssb[J[a]]; sm_ = sm[J[a]]; pn_ = pn[J[a]]
                    ACTF(pn_[:, 0:ntot], s_[:, 0:ntot], AF.Exp, [s_, sm_], [pn_, sm_], bias=sm_[:, 1:2], scale=1.0, accum=sm_[:, 2:3])
                for a in range(2):
                    sm_ = sm[J[a]]
                    RECIP(sm_[:, 3:4], sm_[:, 2:3], [sm_], [sm_])
                for a in range(2):
                    pn_ = pn[J[a]]
                    pv = psTr[a].ap.bitcast(BF16).rearrange("p (a b) -> p a b", a=8)
                    for kt in range(nkt):
                        TRN(pv[:, kt, :], pn_[:, kt * 128:(kt + 1) * 128], [pn_], [psTr[a]])
                for a in range(2):
                    pv = psTr[a].ap.bitcast(BF16).rearrange("p (a b) -> p a b", a=8)
                    CPY("act", pTs[J[a]][:, 0:nkt, :], pv[:, 0:nkt, :], [psTr[a]], [pTs[J[a]]])
                for a in range(2):
                    pt_ = pTs[J[a]]
                    for kt in range(nkt):
                        MM(psO[a][:, 0:64], pt_[:, kt, :], v_[:, vtiles[kt], a * 64:(a + 1) * 64], kt == 0, kt == nkt - 1, [v_, pt_], [psO[a]])
                for a in range(2):
                    sm_ = sm[J[a]]
                    TS("dve", o_[:, q0 // 128, a * 64:(a + 1) * 64], psO[a][:, 0:64], sm_[:, 3:4], None, ALU.mult, None, [psO[a], sm_], [o_])

            for m in range(8):
                q_ = qTm[m % 2]; k_ = kTm[m % 2]; v_ = vm[m % 2]; o_ = osb[m % 2]; bt = btab[m % 2]
                P.dma(q_[:], qT_scr[m], reads=[D_qT], writes=[q_])
                P.dma(k_[:], kT_scr[m], reads=[D_kT], writes=[k_])
                for t0 in range(0, NT, 6):
                    t1_ = min(NT, t0 + 6)
                    P.dma(v_[:, t0:t1_, :], v_scr.rearrange("(t p) d -> p t d", p=128)[:, t0:t1_, m * 128:(m + 1) * 128], reads=[D_v], writes=[v_], q="act")
                for a in range(2):
                    P.dma(bt[:, a * 5:(a + 1) * 5, :], na_bias[2 * m + a].rearrange("t p k -> p t k"), reads=[], writes=[bt], q="act")
                for p in range(32):
                    ws = min(max(2 * p - 4, 0), 54)
                    typ = 0 if p == 0 else 1 if p == 1 else 3 if p == 30 else 4 if p == 31 else 2
                    segs = [(ws * 64, 512), (ws * 64 + 512, 128), (NLAT, 256)]
                    vt = [ws // 2 + i for i in range(5)] + [32, 33]
                    attend2(m, p * 128, segs, [bt[:, a * 5 + typ, :] for a in range(2)], bt, vt)
                ntile = NT if want_ctx else 32
                if want_ctx:
                    for ct in range(2):
                        attend2(m, NLAT + ct * 128, [(NLAT, 256)], None, bt, [32, 33])
                for t0 in range(0, ntile, 6):
                    t1_ = min(ntile, t0 + 6)
                    P.dma(o_scr.rearrange("(t p) d -> p t d", p=128)[:, t0:t1_, m * 128:(m + 1) * 128], o_[:, t0:t1_, :], reads=[o_], writes=[D_o], q="pool")
            out_proj_phase(L, want_ctx, tokmajor=True)

        def out_proj_phase(L, want_ctx, tokmajor=False):
            new_phase()
            bc = {}
            for r in range(2 if want_ctx else 1):
                bc[r] = A.alloc([128, D], F32, f"bcg{r}")
                load_bc(bc[r], L, r, 2, q="act")
            g1 = A.alloc([128, D], F32, "g1"); b1 = A.alloc([128, D], F32, "b1")
            load_row_bc(g1, din["post_ln_g"][L, 0:1, :], q="act")
            load_row_bc(b1, din["post_ln_b"][L, 0:1, :], q="act")
            wo = A.alloc([128, 8, D], BF16, "wo")
            P.dma(wo[:], wob.rearrange("(c p) n -> p c n", p=128), reads=[D_wob], writes=[wo])
            oT = A.ring(2, [128, 8, 512], BF16, "oT")
            otk = A.ring(2, [128, D], BF16, "otk")
            ebufs = epilogue_bufs()
            psYs = [(pst[2], (bank[4], bank[5])), (pst[3], (bank[6], bank[7]))]
            sts = [(s * 512, 512, 0) for s in range(8)]
            if want_ctx:
                sts.append((NLAT, 256, 1))
            k = 0
            for si, (r0, ntok, r) in enumerate(sts):
                nt = ntok // 128
                o_ = oT[si % 2]
                if tokmajor:
                    for t in range(nt):
                        ot_ = otk[t % 2]
                        P.dma(ot_[:], o_scr[r0 + t * 128:r0 + (t + 1) * 128, :], reads=[D_o], writes=[ot_])
                        pv = bank[t % 2].ap.bitcast(BF16).rearrange("p (a b) -> p a b", a=8)
                        for c in range(8):
                            TRN(pv[:, c, :], ot_[:, c * 128:(c + 1) * 128], [ot_], [bank[t % 2]])
                        CPY("act", o_[:, :, t * 128:(t + 1) * 128], pv, [bank[t % 2]], [o_])
                else:
                    P.dma(o_[:, :, 0:ntok], oT_scr.rearrange("m p t -> p m t")[:, :, r0:r0 + ntok], reads=[D_oT], writes=[o_])
                for t in range(nt):
                    psYap, psYb = psYs[k % 2]
                    for n in range(2):
                        for m in range(8):
                            MM(psYap[:, n * 512:(n + 1) * 512], o_[:, m, t * 128:(t + 1) * 128], wo[:, m, n * 512:(n + 1) * 512], m == 0, m == 7, [o_, wo], [psYb[n]])
                    mixer_epilogue(L, r0, t, psYap, psYb, bc[r], g1, b1, ebufs, k)
                    k += 1

        def rwkv_phase(L, slot, want_ctx):
            pw = {n: din[n][slot] for n in W_NAMES if n.startswith("rw_")}
            new_phase()
            r32 = A.ring(3, [128, 2048], F32, "c32")
            r16 = A.ring(3, [128, 2048], BF16, "c16")
            ctr = [0]
            cast_weight(pw["rw_wr"], wqkvb[:, 0:1024], D_wqkvb, D, D, r32, r16, ctr)
            cast_weight(pw["rw_wk"], wqkvb[:, 1024:2048], D_wqkvb, D, D, r32, r16, ctr)
            cast_weight(pw["rw_wv"], wqkvb[:, 2048:3072], D_wqkvb, D, D, r32, r16, ctr)
            cast_weight(pw["rw_wo"], wob, D_wob, D, D, r32, r16, ctr)
            new_phase()
            bc = {}
            for r in range(2):
                for j in (0, 1):
                    bc[(r, j)] = A.alloc([128, D], F32, f"bc{r}{j}")
                    load_bc(bc[(r, j)], L, r, j, q="act")
            xr = A.ring(3, [128, D], F32, "xr"); hm = A.ring(3, [128, D], F32, "hm")
            for t in range(NT):
                r = 0 if t < 32 else 1
                xt = xr[t % 3]; h_ = hm[t % 3]
                sap, sbuf_ = src_rows(L, t * 128, 128)
                P.dma(xt[:], sap, reads=[sbuf_], writes=[xt])
                TT("dve", h_[:], xt[:], bc[(r, 1)][:], ALU.mult, [xt, bc[(r, 1)]], [h_])
                TT("pool", h_[:], h_[:], bc[(r, 0)][:], ALU.add, [h_, bc[(r, 0)]], [h_])
                P.dma(rws["hs"][t * 128:(t + 1) * 128, :], h_[:], reads=[h_], writes=[D_rw["hs"]], q="pool")
            if rw_stop < 1:
                return
            new_phase()
            wrkv = A.alloc([128, 8, 3072], BF16, "wrkv")
            P.dma(wrkv[:], wqkvb.rearrange("(c p) n -> p c n", p=128), reads=[D_wqkvb], writes=[wrkv])
            xx = A.alloc([128, D], F32, "xx")
            stg = Buf(xx.ap.rearrange("p (n c r) -> p n c r", n=2, c=8), "stg"); stg = xx.__class__(stg.ap, "stg") if False else xx
            stg_ap = xx.ap.rearrange("p (n c r) -> p n c r", n=2, c=8)
            w1b_ = A.alloc([128, 2, 8, 64], BF16, "w1b_"); a1b_ = A.alloc([128, 2, 8, 64], BF16, "a1b_")
            for (nm, dst) in (("rw_w1", w1b_), ("rw_a1", a1b_)):
                for n in range(2):
                    P.dma(stg_ap[:, n, :, :], pw[nm][n].rearrange("(c p) r -> p c r", p=128), reads=[], writes=[xx])
                CPY("pool", dst[:], stg_ap, [xx], [dst])
            stg2 = A.alloc([128, 2, D], F32, "hpn")
            w2b_ = A.alloc([128, 2, D], BF16, "w2b_"); a2b_ = A.alloc([128, 2, D], BF16, "a2b_")
            for (nm, dst) in (("rw_w2", w2b_), ("rw_a2", a2b_)):
                P.dma(stg2[0:64, :, :], pw[nm].rearrange("n r d -> r n d"), reads=[], writes=[stg2])
                CPY("pool", dst[0:64], stg2[0:64], [stg2], [dst])
            g1b_ = A.alloc([128, 8, 128], BF16, "g1b_"); g2b_ = A.alloc([128, D], BF16, "g2b_")
            P.dma(stg2[:, 0, :].rearrange("p (c r) -> p c r", c=8), pw["rw_g1"].rearrange("(c p) r -> p c r", p=128), reads=[], writes=[stg2])
            CPY("pool", g1b_[:], stg2[:, 0, :].rearrange("p (c r) -> p c r", c=8), [stg2], [g1b_])
            P.dma(stg2[:, 1, :], pw["rw_g2"], reads=[], writes=[stg2])
            CPY("pool", g2b_[:], stg2[:, 1, :], [stg2], [g2b_])
            mu = A.ring(6, [128, D], F32, "mu")
            for j in range(6):
                load_row_bc(mu[j], pw["rw_mu"][j:j + 1, :], q="act")
            vb = {}
            for nm, ap_ in (("w00", pw["rw_w0"][0:1, :]), ("w01", pw["rw_w0"][1:2, :]), ("a00", pw["rw_a0"][0:1, :]), ("a01", pw["rw_a0"][1:2, :]),
                            ("kkb", pw["rw_k_k"].rearrange("(o d) -> o d", o=1)), ("kab", pw["rw_k_a"].rearrange("(o d) -> o d", o=1)),
                            ("rkb", pw["rw_r_k"].rearrange("(o h) n -> o (h n)", o=1))):
                vb[nm] = A.alloc([128, D], F32, nm)
                load_row_bc(vb[nm], ap_, q="act")
            omka = A.alloc([128, D], F32, "omka")
            TS("dve", omka[:], vb["kab"][:], -1.0, 1.0, ALU.mult, ALU.add, [vb["kab"]], [omka])
            hc1 = A.alloc([128, D], F32, "hc")
            xm = A.ring(2, [128, D], F32, "xm"); xj = A.ring(2, [128, D], BF16, "xj")
            xT = A.ring(6, [128, 8, 128], BF16, "xT")
            rsb = A.alloc([128, D], F32, "rsb"); ksb = A.alloc([128, D], F32, "ksb")
            kks = A.alloc([128, D], F32, "kks"); rbs = A.alloc([128, D], F32, "rbs")
            t1s = A.alloc([128, D], F32, "t1"); t1 = [t1s, t1s]; t2 = A.ring(2, [128, D], F32, "t2"); t3 = A.ring(2, [128, D], F32, "t3")
            gsb = t2[0]; vsb_ = t3[1]
            lth = A.ring(2, [128, 128], BF16, "lth")
            s16 = A.ring(2, [128, 64], F32, "s16")
            psT = bank[0]; psl = [bank[1]]
            big = [(pst[1], (bank[2], bank[3])), (pst[2], (bank[4], bank[5])), (pst[3], (bank[6], bank[7]))]
            bi = 0
            for t in range(NT):
                k = t % 2
                r0 = t * 128
                first = (t == 0 or t == 32); last = (t == 31 or t == 33)
                c_ = hc1
                pA = stg2[:, 0, :]; nA = stg2[:, 1, :]
                P.dma(c_[:], rws["hs"][r0:r0 + 128, :], reads=[D_rw["hs"]], writes=[c_])
                if first:
                    P.pool(lambda e: e.memset(stg2[0:1, 0, :], 0.0), [], [stg2])
                    P.dma(stg2[1:128, 0, :], rws["hs"][r0:r0 + 127, :], reads=[D_rw["hs"]], writes=[stg2])
                else:
                    P.dma(pA, rws["hs"][r0 - 1:r0 + 127, :], reads=[D_rw["hs"]], writes=[stg2])
                if last:
                    P.pool(lambda e: e.memset(stg2[96:128, 1, :], 0.0), [], [stg2])
                    P.dma(stg2[0:127, 1, :], rws["hs"][r0 + 1:r0 + 128, :], reads=[D_rw["hs"]], writes=[stg2], q="act")
                else:
                    P.dma(nA, rws["hs"][r0 + 1:r0 + 129, :], reads=[D_rw["hs"]], writes=[stg2], q="act")
                TT("pool", pA, pA, nA, ALU.add, [stg2], [stg2])
                STT("dve", xx[:], pA, 0.5, c_[:], ALU.mult, ALU.subtract, [stg2, c_], [xx])
                for j in range(6):
                    m_ = xm[j % 2]; x_ = xj[j % 2]
                    TT("pool", m_[:], xx[:], mu[j][:], ALU.mult, [xx, mu[j]], [m_])
                    TT("dve" if j % 2 else "pool", x_[:], m_[:], c_[:], ALU.add, [m_, c_], [x_])
                    pv = psT.ap.bitcast(BF16).rearrange("p (a b) -> p a b", a=8)
                    for c in range(8):
                        TRN(pv[:, c, :], x_[:, c * 128:(c + 1) * 128], [x_], [psT])
                    CPY("act", xT[j][:], pv, [psT], [xT[j]])

                def proj(j, col0, dst):
                    nonlocal bi
                    pa, pb = big[bi % 3]; bi += 1
                    for n in range(2):
                        for c in range(8):
                            MM(pa[:, n * 512:(n + 1) * 512], xT[j][:, c, :], wrkv[:, c, col0 + n * 512:col0 + (n + 1) * 512], c == 0, c == 7, [xT[j], wrkv], [pb[n]])
                    CPY("act", dst[:], pa[:, :], list(pb), [dst])

                proj(0, 0, rsb); proj(2, 1024, ksb); proj(3, 2048, vsb_)
                P.dma(rws["r"][r0:r0 + 128, :], rsb[:], reads=[rsb], writes=[D_rw["r"]], q="pool")
                P.dma(rws["v"][r0:r0 + 128, :], vsb_[:], reads=[vsb_], writes=[D_rw["v"]], q="pool")
                pg = psl[0]
                for c in range(8):
                    MM(pg[:, 0:128], g1b_[:, c, :], xT[5][:, c, :], c == 0, c == 7, [g1b_, xT[5]], [pg])
                lg = lth[0]
                ACTF(lg[:], pg[:, 0:128], AF.Sigmoid, [pg], [lg])
                pa, pb = big[bi % 3]; bi += 1
                for n in range(2):
                    MM(pa[:, n * 512:(n + 1) * 512], lg[:], g2b_[:, n * 512:(n + 1) * 512], True, True, [lg, g2b_], [pb[n]])
                CPY("act", gsb[:], pa[:, :], list(pb), [gsb])
                P.dma(rws["g"][r0:r0 + 128, :], gsb[:], reads=[gsb], writes=[D_rw["g"]], q="pool")
                TT("pool", kks[:], ksb[:], vb["kkb"][:], ALU.mult, [ksb, vb["kkb"]], [kks])
                q1 = t1[0]
                TT("pool", q1[:], kks[:], kks[:], ALU.mult, [kks], [q1])
                sv = s16[t % 2]
                P.dve(lambda e, sv=sv, q1=q1: e.tensor_reduce(out=sv[:, 0:16], in_=q1[:].rearrange("p (h n) -> p h n", h=16), axis=AX.X, op=ALU.add), [q1], [sv])
                TS("dve", sv[:, 0:16], sv[:, 0:16], 1e-12, None, ALU.max, None, [sv], [sv])
                ACTF(sv[:, 16:32], sv[:, 0:16], AF.Sqrt, [sv], [sv])
                RECIP(sv[:, 32:48], sv[:, 16:32], [sv], [sv])
                for hd in range(16):
                    TS("pool", kks[:, hd * 64:(hd + 1) * 64], kks[:, hd * 64:(hd + 1) * 64], sv[:, 32 + hd:33 + hd], None, ALU.mult, None, [kks, sv], [kks])
                P.dma(rws["kk"][r0:r0 + 128, :], kks[:], reads=[kks], writes=[D_rw["kk"]], q="pool")
                TT("pool", rbs[:], rsb[:], vb["rkb"][:], ALU.mult, [rsb, vb["rkb"]], [rbs])
                for d in range(2):
                    pl = psl[0]
                    for c in range(8):
                        MM(pl[0:64, 0:128], w1b_[:, d, c, :], xT[1][:, c, :], c == 0, c == 7, [w1b_, xT[1]], [pl])
                    lt = lth[1]
                    ACTF(lt[0:64, :], pl[0:64, 0:128], AF.Tanh, [pl], [lt])
                    pa, pb = big[bi % 3]; bi += 1
                    for n in range(2):
                        MM(pa[:, n * 512:(n + 1) * 512], lt[0:64, :], w2b_[0:64, d, n * 512:(n + 1) * 512], True, True, [lt, w2b_], [pb[n]])
                    u1 = t1[1]; u2 = t2[d]
                    TT("dve", u1[:], pa[:, :], vb[f"w0{d}"][:], ALU.add, list(pb) + [vb[f"w0{d}"]], [u1])
                    ACTF(u1[:], u1[:], AF.Exp, [u1], [u1], scale=-1.0)
                    ACTF(u1[:], u1[:], AF.Ln, [u1], [u1], bias=1.0)
                    ACTF(u1[:], u1[:], AF.Exp, [u1], [u1], scale=-1.0, bias=-0.5)
                    TS("pool", u2[:], u1[:], -1.0, None, ALU.mult, None, [u1], [u2])
                    P.dma(rws[f"lw{d}"][r0:r0 + 128, :], u2[:], reads=[u2], writes=[D_rw[f"lw{d}"]], q="pool")
                    pl = psl[0]
                    for c in range(8):
                        MM(pl[0:64, 0:128], a1b_[:, d, c, :], xT[4][:, c, :], c == 0, c == 7, [a1b_, xT[4]], [pl])
                    lt = lth[1]
                    CPY("act", lt[0:64, :], pl[0:64, 0:128], [pl], [lt])
                    pa, pb = big[bi % 3]; bi += 1
                    for n in range(2):
                        MM(pa[:, n * 512:(n + 1) * 512], lt[0:64, :], a2b_[0:64, d, n * 512:(n + 1) * 512], True, True, [lt, a2b_], [pb[n]])
                    a_ = t3[0]; b_ = t3[1]
                    TT("dve", a_[:], pa[:, :], vb[f"a0{d}"][:], ALU.add, list(pb) + [vb[f"a0{d}"]], [a_])
                    ACTF(a_[:], a_[:], AF.Sigmoid, [a_], [a_])
                    TT("pool", b_[:], kks[:], a_[:], ALU.mult, [kks, a_], [b_])
                    P.dma(rws[f"b{d}"][r0:r0 + 128, :], b_[:], reads=[b_], writes=[D_rw[f"b{d}"]], q="pool")
                    TT("pool", a_[:], a_[:], vb["kab"][:], ALU.mult, [a_, vb["kab"]], [a_])
                    TT("pool", a_[:], a_[:], omka[:], ALU.add, [a_, omka], [a_])
                    kd_ = xm[d]
                    TT("pool", kd_[:], ksb[:], a_[:], ALU.mult, [ksb, a_], [kd_])
                    P.dma(rws[f"kd{d}"][r0:r0 + 128, :], kd_[:], reads=[kd_], writes=[D_rw[f"kd{d}"]], q="pool")
                    TT("pool", a_[:], rbs[:], kd_[:], ALU.mult, [rbs, kd_], [a_])
                    so = sv[:, 0:16] if d == 0 else sv[:, 16:32]
                    P.dve(lambda e, so=so, a_=a_: e.tensor_reduce(out=so, in_=a_[:].rearrange("p (h n) -> p h n", h=16), axis=AX.X, op=ALU.add), [a_], [sv])
                TT("dve", sv[:, 32:48], sv[:, 0:16], sv[:, 16:32], ALU.add, [sv], [sv])
                P.dma(bon_scr[r0:r0 + 128, :], sv[:, 32:48], reads=[sv], writes=[D_bon], q="pool")
            if rw_stop < 2:
                return
            for d in range(2):
                if rw_stop < 3 and d == 1:
                    return
                new_phase()
                rwc = A.alloc([128, 1282], F32, "rwc")
                P.dma(rwc[:], rwc_in[:, :], reads=[], writes=[rwc])
                o_ = d * 640
                TRI = rwc[:, o_:o_ + 128]; TRI2 = rwc[:, o_ + 128:o_ + 256]; MSI = rwc[:, o_ + 256:o_ + 512]; MLT = rwc[:, o_ + 512:o_ + 640]
                CH = rwc[:, 1280:1282]
                m4m = A.alloc([128, 512], F32, "m4m")
                CPY("pool", m4m[:, 0:256], MSI, [rwc], [m4m])
                CPY("pool", m4m[:, 256:512], MSI, [rwc], [m4m])
                names = ["r", "v", "kk", f"kd{d}", f"b{d}", f"lw{d}"]
                ld = {n: A.ring(2, [128, D], F32, "ld" + n) for n in names}
                ecum = A.alloc([128, D], F32, "ecum"); encum = A.alloc([128, D], F32, "encum"); eE = A.alloc([128, D], F32, "eE"); ecm = A.alloc([128, D], F32, "ecm")
                At = A.alloc([128, D], F32, "At"); Rt = A.alloc([128, D], F32, "Rt")
                Kt16 = A.alloc([128, D], BF16, "Kt16"); Bt16 = A.alloc([128, D], BF16, "Bt16")
                Khr = A.ring(2, [128, D], F32, "Kh"); Bhr = A.ring(2, [128, D], F32, "Bh")
                Vpr = A.ring(2, [128, 2, D], F32, "Vp")
                FKr = A.ring(2, [128, 8, 128], BF16, "FK"); FBr = A.ring(2, [128, 8, 128], BF16, "FB")
                AR16r = [A.ring(2, [128, 8, 256], BF16, f"AR16{p}") for p in range(2)]
                AR32r = [A.ring(1, [128, 8, 256], F32, f"AR32{p}") for p in range(2)]
                gamr = A.ring(2, [128, 16], F32, "gam")
                Hp = A.alloc([128, 16, 64], F32, "Hp")
                M4 = A.ring(4, [128, 512], F32, "M4")
                L16 = A.ring(8, [128, 128], BF16, "L16"); P16 = A.ring(8, [128, 128], BF16, "P16"); Q16 = A.ring(8, [128, 128], BF16, "Q16")
                Q32 = A.ring(4, [128, 128], F32, "Q32"); LMVr = A.ring(4, [128, 128], F32, "LMV")
                Wp = A.ring(8, [128, 64], F32, "Wp"); Up = A.ring(8, [128, 64], F32, "Up"); tYr = A.ring(8, [128, 64], F32, "tY")
                Yt = A.ring(2, [128, D], F32, "Yt")
                for z in [Hp] + Vpr + AR16r[0] + AR16r[1] + AR32r[0] + AR32r[1] + Wp + Up:
                    P.pool(lambda e, z=z: e.memset(z[:], 0.0), [], [z])
                HSb = [Buf(Hp.ap[:, h, :], f"HS{h}") for h in range(16)]
                for hb_ in HSb:
                    hb_.w = Hp.w
                order = [32, 33] + list(range(32)) if d == 0 else [33, 32] + list(range(31, -1, -1))
                halves = (0, 1) if d == 0 else (1, 0)
                for ti, t in enumerate(order):
                    r0 = t * 128
                    cur = {}
                    for qi, n in enumerate(names):
                        b_ = ld[n][ti % 2]
                        P.dma(b_[:], rws[n][r0:r0 + 128, :], reads=[D_rw[n]], writes=[b_], q="sp" if qi % 2 == 0 else "act")
                        cur[n] = b_
                    r_, v_, kk_, kd_, bb_, lw_ = [cur[n] for n in names]
                    Kh = Khr[ti % 2]; Bh = Bhr[ti % 2]; Vp = Vpr[ti % 2]; FK = FKr[ti % 2]; FB = FBr[ti % 2]; gam = gamr[ti % 2]
                    AR16 = [AR16r[0][ti % 2], AR16r[1][ti % 2]]; AR32 = [AR32r[0][0], AR32r[1][0]]
                    for n in range(2):
                        MM(pst[0][:, n * 512:(n + 1) * 512], TRI, lw_[:, n * 512:(n + 1) * 512], True, True, [rwc, lw_], [bank[n]])
                    for n in range(2):
                        MM(pst[1][:, n * 512:(n + 1) * 512], TRI2, lw_[:, n * 512:(n + 1) * 512], True, True, [rwc, lw_], [bank[2 + n]])
                    for m in range(8):
                        MM(bank[5][:, 2 * m:2 * m + 2], lw_[:, m * 128:(m + 1) * 128], CH, True, True, [lw_, rwc], [bank[5]])
                    pb = [bank[0], bank[1]]; pb2 = [bank[2], bank[3]]
                    ACTF(ecum[:], pst[0][:, :], AF.Exp, pb, [ecum])
                    ACTF(encum[:], pst[0][:, :], AF.Exp, pb, [encum], scale=-1.0)
                    TT("dve", ecm[:], pst[0][:, :], lw_[:], ALU.subtract, pb + [lw_], [ecm])
                    ACTF(eE[:], pst[1][:, :], AF.Exp, pb2, [eE])
                    ACTF(ecm[:], ecm[:], AF.Exp, [ecm], [ecm])
                    ACTF(gam[:], bank[5][:, 0:16], AF.Exp, [bank[5]], [gam])
                    TT("pool", Rt[:], r_[:], ecum[:], ALU.mult, [r_, ecum], [Rt])
                    STT("dve", At[:], kk_[:], -1.0, ecm[:], ALU.mult, ALU.mult, [kk_, ecm], [At])
                    TT("pool", Kt16[:], kd_[:], encum[:], ALU.mult, [kd_, encum], [Kt16])
                    TT("pool", Bt16[:], bb_[:], encum[:], ALU.mult, [bb_, encum], [Bt16])
                    TT("pool", Kh[:], kd_[:], eE[:], ALU.mult, [kd_, eE], [Kh])
                    TT("dve", Bh[:], bb_[:], eE[:], ALU.mult, [bb_, eE], [Bh])
                    CPY("pool", Vp[0:64, 0, :], v_[0:64, :], [v_], [Vp])
                    CPY("pool", Vp[64:128, 1, :], v_[64:128, :], [v_], [Vp])
                    for (src, dstF, bk) in ((Kt16, FK, bank[6]), (Bt16, FB, bank[7])):
                        pv = bk.ap.bitcast(BF16).rearrange("p (a b) -> p a b", a=8)
                        for m in range(8):
                            TRN(pv[:, m, :], src[:, m * 128:(m + 1) * 128], [src], [bk])
                        CPY("act", dstF[:], pv, [bk], [dstF])
                    for mp in range(4):
                        bk = bank[4 + mp % 2]
                        for mi in range(2):
                            m = 2 * mp + mi
                            for ki, src in enumerate((At, Rt)):
                                c0 = mi * 256 + ki * 128
                                P.pe(lambda e, bk=bk, c0=c0, src=src, m=m: e.transpose(bk[:, c0:c0 + 128], src[:, m * 128:(m + 1) * 128], ident32[:]), [src, ident32], [bk])
                        for par in range(2):
                            pr = slice(par * 64, par * 64 + 64)
                            CPY("act", AR32[par][pr, 2 * mp:2 * mp + 2, :], bk[pr, :].rearrange("p (a b) -> p a b", a=2), [bk], [AR32[par]])
                            CPY("dve", AR16[par][pr, 2 * mp:2 * mp + 2, :], bk[pr, :].rearrange("p (a b) -> p a b", a=2), [bk], [AR16[par]])
                    Y_ = Yt[ti % 2]
                    for g in range(4):
                        hs = [4 * g + i for i in range(4)]

                        def reg(i, k):
                            return bank[5 + (1, 0, 2)[k]], i * 128

                        Mh = [M4[i] for i in range(4)]
                        cur_L = [L16[2 * i] for i in range(4)]; cur_P = [P16[2 * i] for i in range(4)]; cur_Q = [Q16[2 * i] for i in range(4)]
                        for i, h in enumerate(hs):
                            m = h // 2; par = h % 2
                            pq = bank[i]
                            MM(pq[:, 0:256], FB[:, m, :], AR16[par][:, m, :], True, True, [FB, AR16[par]], [pq])
                            MM(pq[:, 256:512], FK[:, m, :], AR16[par][:, m, :], True, True, [FK, AR16[par]], [pq])
                            MM(bank[4][:, i * 128:(i + 1) * 128], AR16[par][:, m, 0:128], FB[:, m, :], True, True, [FB, AR16[par]], [bank[4]])
                        for i, h in enumerate(hs):
                            TT("dve", Mh[i][:], bank[i][:, :], m4m[:], ALU.mult, [bank[i], m4m], [Mh[i]])
                            TT("dve", cur_L[i][:], bank[4][:, i * 128:(i + 1) * 128], MLT, ALU.mult, [bank[4], rwc], [cur_L[i]])
                            CPY("act", cur_P[i][:], Mh[i][:, 0:128], [Mh[i]], [cur_P[i]])
                            TT("pool", cur_Q[i][:], Mh[i][:, 0:128], ident32[:], ALU.add, [Mh[i], ident32], [cur_Q[i]])
                        for lev in range(1, 6):
                            nL = [L16[2 * i + lev % 2] for i in range(4)]; nP = [P16[2 * i + lev % 2] for i in range(4)]
                            nQ = [Q16[2 * i + lev % 2] for i in range(4)] if lev < 5 else [Q32[i] for i in range(4)]
                            for i in range(4):
                                b1, c1 = reg(i, 1)
                                MM(b1[:, c1:c1 + 128], cur_P[i][:], cur_L[i][:], True, True, [cur_L[i], cur_P[i]], [b1])
                                if lev < 5:
                                    b0, c0 = reg(i, 0)
                                    MM(b0[:, c0:c0 + 128], cur_L[i][:], cur_P[i][:], True, True, [cur_L[i], cur_P[i]], [b0])
                            for i in range(4):
                                b1, c1 = reg(i, 1)
                                CPY("act", nL[i][:], b1[:, c1:c1 + 128], [b1], [nL[i]])
                                if lev < 5:
                                    b0, c0 = reg(i, 0)
                                    CPY("dve", nP[i][:], b0[:, c0:c0 + 128], [b0], [nP[i]])
                            for i in range(4):
                                b2, c2 = reg(i, 2)
                                MM(b2[:, c2:c2 + 128], nL[i][:], cur_Q[i][:], True, True, [nL[i], cur_Q[i]], [b2])
                            for i in range(4):
                                b2, c2 = reg(i, 2)
                                TT("dve", nQ[i][:], b2[:, c2:c2 + 128], cur_Q[i][:], ALU.add, [b2, cur_Q[i]], [nQ[i]])
                            cur_L, cur_P, cur_Q = nL, nP, nQ
                        Qf = cur_Q
                        for i, h in enumerate(hs):
                            vh = v_[:, h * 64:(h + 1) * 64]
                            MM(bank[4][:, i * 128:i * 128 + 64], Mh[i][:, 256:384], vh, True, True, [Mh[i], v_], [bank[4]])
                            MM(bank[4][:, i * 128 + 64:(i + 1) * 128], Mh[i][:, 384:512], vh, True, True, [Mh[i], v_], [bank[4]])
                        for i in range(4):
                            CPY("act", LMVr[i][:], bank[4][:, i * 128:(i + 1) * 128], [bank[4]], [LMVr[i]])
                        for hp_ in halves:
                            sl = slice(hp_ * 64, hp_ * 64 + 64)
                            for i, h in enumerate(hs):
                                m = h // 2; par = h % 2
                                MM(bank[i][:, 0:64], AR32[par][:, m, 0:128], HSb[h].ap, True, True, [AR32[par], HSb[h]], [bank[i]])
                            for i, h in enumerate(hs):
                                W_ = Wp[2 * i + hp_]
                                TT("dve", W_[sl, :], bank[i][sl, 0:64], LMVr[i][sl, 0:64], ALU.add, [bank[i], LMVr[i]], [W_])
                            for i, h in enumerate(hs):
                                MM(bank[i][:, 64:128], Qf[i][:], Wp[2 * i + hp_][:], True, True, [Qf[i], Wp[2 * i + hp_]], [bank[i]])
                            for i, h in enumerate(hs):
                                CPY("act", Up[2 * i + hp_][sl, :], bank[i][sl, 64:128], [bank[i]], [Up[2 * i + hp_]])
                            for i, h in enumerate(hs):
                                m = h // 2; par = h % 2
                                MM(bank[i][:, 128:192], AR32[par][:, m, 128:256], HSb[h].ap, True, True, [AR32[par], HSb[h]], [bank[i]])
                                MM(bank[i][:, 192:256], Mh[i][:, 128:256], Up[2 * i + hp_][:], True, True, [Mh[i], Up[2 * i + hp_]], [bank[i]])
                                MM(bank[i][:, 256:320], Kh[:, m * 128:(m + 1) * 128], Vp[:, hp_, h * 64:(h + 1) * 64], True, False, [Kh, Vp], [bank[i]])
                                MM(bank[i][:, 256:320], Bh[:, m * 128:(m + 1) * 128], Up[2 * i + hp_][:], False, True, [Bh, Up[2 * i + hp_]], [bank[i]])
                            for i, h in enumerate(hs):
                                tY = tYr[2 * i + hp_]
                                TT("dve", tY[sl, :], bank[i][sl, 128:192], LMVr[i][sl, 64:128], ALU.add, [bank[i], LMVr[i]], [tY])
                                TT("dve", Y_[sl, h * 64:(h + 1) * 64], bank[i][sl, 192:256], tY[sl, :], ALU.add, [bank[i], tY], [Y_])
                            for i, h in enumerate(hs):
                                m = h // 2; jl = (h % 2) * 64; jh = jl + 64
                                STT("dve", Hp[jl:jh, h, :], Hp[jl:jh, h, :], gam[jl:jh, 2 * m + hp_:2 * m + hp_ + 1], bank[i][jl:jh, 256:320],
                                    ALU.mult, ALU.add, [HSb[h], gam, bank[i]], [HSb[h]])
                    P.dma(rws[f"y{d}"][r0:r0 + 128, :], Y_[:], reads=[Y_], writes=[D_rw[f"y{d}"]], q="pool")
            if rw_stop < 4:
                return
            new_phase()
            lg_ = A.alloc([128, D], F32, "lg_"); lb_ = A.alloc([128, D], F32, "lb_")
            load_row_bc(lg_, pw["rw_lnx_g"].rearrange("(o d) -> o d", o=1), q="act")
            load_row_bc(lb_, pw["rw_lnx_b"].rearrange("(o d) -> o d", o=1), q="act")
            y0 = A.ring(2, [128, D], F32, "y0"); y1 = A.ring(2, [128, D], F32, "y1"); vv = A.ring(2, [128, D], F32, "vv"); gg = A.ring(2, [128, D], F32, "gg")
            bn = A.ring(2, [128, 16], F32, "bn")
            st_ = A.ring(2, [128, 16, 6], F32, "st_"); mv_ = A.ring(2, [128, 16, 2], F32, "mv_"); rs_ = A.ring(2, [128, 32], F32, "rs_")
            ob = A.ring(2, [128, D], BF16, "ob"); oTt = A.ring(2, [128, 8, 128], BF16, "oTt")
            nt_out = NT if want_ctx else 32
            for t in range(nt_out):
                k = t % 2; r0 = t * 128
                a_, b_, v_, g_, bo = y0[k], y1[k], vv[k], gg[k], bn[k]
                P.dma(a_[:], rws["y0"][r0:r0 + 128, :], reads=[D_rw["y0"]], writes=[a_])
                P.dma(b_[:], rws["y1"][r0:r0 + 128, :], reads=[D_rw["y1"]], writes=[b_], q="act")
                P.dma(v_[:], rws["v"][r0:r0 + 128, :], reads=[D_rw["v"]], writes=[v_])
                P.dma(g_[:], rws["g"][r0:r0 + 128, :], reads=[D_rw["g"]], writes=[g_], q="act")
                P.dma(bo[:], bon_scr[r0:r0 + 128, :], reads=[D_bon], writes=[bo], q="act")
                TT("pool", a_[:], a_[:], b_[:], ALU.add, [a_, b_], [a_])
                s_ = st_[k]; m_ = mv_[k]; q_ = rs_[k]
                for hd in range(16):
                    P.dve(lambda e, s_=s_, a_=a_, hd=hd: e.bn_stats(out=s_[:, hd, :], in_=a_[:, hd * 64:(hd + 1) * 64]), [a_], [s_])
                    P.dve(lambda e, s_=s_, m_=m_, hd=hd: e.bn_aggr(out=m_[:, hd, :], in_=s_[:, hd:hd + 1, :]), [s_], [m_])
                ACTF(q_[:, 0:16], m_[:, :, 1], AF.Sqrt, [m_], [q_], bias=64e-5)
                RECIP(q_[:, 16:32], q_[:, 0:16], [q_], [q_])
                for hd in range(16):
                    TS("dve" if hd % 2 else "pool", b_[:, hd * 64:(hd + 1) * 64], a_[:, hd * 64:(hd + 1) * 64], m_[:, hd, 0:1], q_[:, 16 + hd:17 + hd], ALU.subtract, ALU.mult, [a_, m_, q_], [b_])
                TT("pool", b_[:], b_[:], lg_[:], ALU.mult, [b_, lg_], [b_])
                TT("pool", b_[:], b_[:], lb_[:], ALU.add, [b_, lb_], [b_])
                for hd in range(16):
                    STT("dve", b_[:, hd * 64:(hd + 1) * 64], v_[:, hd * 64:(hd + 1) * 64], bo[:, hd:hd + 1], b_[:, hd * 64:(hd + 1) * 64], ALU.mult, ALU.add, [v_, bo, b_], [b_])
                o16 = ob[k]
                TT("pool", o16[:], b_[:], g_[:], ALU.mult, [b_, g_], [o16])
                ptb = bank[k]
                pv = ptb.ap.bitcast(BF16).rearrange("p (a b) -> p a b", a=8)
                for c in range(8):
                    TRN(pv[:, c, :], o16[:, c * 128:(c + 1) * 128], [o16], [ptb])
                ot_ = oTt[k]
                CPY("act", ot_[:], pv, [ptb], [ot_])
                P.dma(oT_scr.rearrange("m p t -> p m t")[:, :, r0:r0 + 128], ot_[:], reads=[ot_], writes=[D_oT], q="pool")
            out_proj_phase(L, want_ctx)

        setup_modulation()
        if rw_only:
            rwkv_phase(0, 0, True)
            n_layers = 0
        for L in range(n_layers):
            kind, slot = L % 3, L // 3
            want_ctx = L < DEPTH - 1
            if kind == 0:
                gqa_phase(L, slot, want_ctx)
            elif kind == 1:
                na_phase(L, slot, want_ctx)
            else:
                rwkv_phase(L, slot, want_ctx)
            mlp_phase(L, want_ctx, final=(L == DEPTH - 1))
        if debug and n_layers > 0:
            new_phase()
            cp = A.ring(3, [128, D], F32, "cp")
            Lr = n_layers - 1
            for t in range(NT):
                b = cp[t % 3]
                P.dma(b[:], xbuf[Lr % 2][t * 128:(t + 1) * 128, :], reads=[D_xb[Lr % 2]], writes=[b])
                P.dma(dbg_out[t * 128:(t + 1) * 128, :], b[:], reads=[b], writes=[D_dbg], q="pool")
        print("instr counts", P.cnt, "dma", {k: v // 16 for k, v in P.dma_val.items()} if False else "")
        P.finish(st)
    return nc


def make_in_maps(inputs, cores):
    cos, sin = rope_tables()
    nab = na_bias_table(np.asarray(inputs["na_rpb"][0], np.float32))
    rwc = rwkv_consts()
    maps = []
    for b in cores:
        m = {
            "x": np.ascontiguousarray(inputs["x"][b]),
            "ctx": np.ascontiguousarray(inputs["ctx"][b]),
            "cc": np.ascontiguousarray(np.stack([inputs["c"][b], inputs["c_ctx"]], axis=0)),
            "rope_cos": cos, "rope_sin": sin, "na_bias": nab, "rwc": rwc,
        }
        for n in W_NAMES:
            m[n] = np.ascontiguousarray(inputs[n])
        maps.append(m)
    return maps


def kernel(**inputs):
    inputs = {k: np.asarray(v) for k, v in inputs.items()}
    nc = build()
    cores = list(range(8))
    res = run_bass_kernel_spmd(nc, make_in_maps(inputs, cores), core_ids=cores)
    return np.stack([r["y"] for r in res.results], axis=0).astype(np.float32)
```
